# Optimizing a Trainium2 kernel written in Bass

```python
import jax, jax.numpy as jnp
from jax import lax
import numpy as np

D_MODEL = 1024
BATCH = 16
SEQ = 2048
DEPTH = 4

GRID_W = 64
CTX_LEN = 256

MLA_HEADS = 8
MLA_Q_LORA = 384
MLA_KV_LORA = 256
MLA_NOPE = 64
MLA_ROPE = 32
MLA_V = 64
MLA_WIDTH = MLA_HEADS * MLA_V
ROPE_BASE = 10000.0
Q_BLOCK = 128

CONV_WIDTH = 512
CONV_K = 3

RW_HEADS = 8
RW_HEAD = 64
RW_WIDTH = RW_HEADS * RW_HEAD
RW_DECAY_LORA = 64
RW_ICLR_LORA = 64
RW_GATE_LORA = 128
RW_GN_EPS = 64e-5
RW_IN = 3 * RW_WIDTH + 2 * RW_DECAY_LORA + 2 * RW_ICLR_LORA + RW_GATE_LORA

N_BRANCH = 3
IN_SPLITS = (MLA_Q_LORA, MLA_KV_LORA, MLA_ROPE, CONV_WIDTH, CONV_WIDTH, CONV_WIDTH, RW_IN, N_BRANCH * D_MODEL)
D_IN = MLA_Q_LORA + MLA_KV_LORA + MLA_ROPE + 3 * CONV_WIDTH + RW_IN + N_BRANCH * D_MODEL

D_FF = -(-(8 * D_MODEL) // (3 * 256)) * 256

LN_EPS = 1e-5
RMS_EPS = 1e-6

kernel_name = "hybrid_mla_conv_rwkv7_dit_prefix"

F32 = jnp.float32


def _split(x, sizes):
    offs = np.cumsum(sizes)[:-1].tolist()
    return jnp.split(x, offs, axis=-1)


def layer_norm(x, g, b):
    xf = x.astype(F32)
    mu = jnp.mean(xf, -1, keepdims=True)
    var = jnp.mean(jnp.square(xf - mu), -1, keepdims=True)
    return ((xf - mu) * lax.rsqrt(var + LN_EPS) * g + b).astype(x.dtype)


def rms_norm(x, g):
    xf = x.astype(F32)
    return (xf * lax.rsqrt(jnp.mean(jnp.square(xf), -1, keepdims=True) + RMS_EPS) * g).astype(x.dtype)


def modulate(x, shift, scale):
    return x * (1.0 + scale) + shift


def axial_rope_angles(seq_len):
    rows = seq_len // GRID_W
    row = jnp.repeat(jnp.arange(rows), GRID_W).astype(F32)
    col = jnp.tile(jnp.arange(GRID_W), rows).astype(F32)
    axis_dim = MLA_ROPE // 2
    inv = ROPE_BASE ** (-jnp.arange(0, axis_dim, 2, dtype=F32) / axis_dim)
    return row[:, None] * inv, col[:, None] * inv


def _rot(x, ang):
    x1, x2 = jnp.split(x, 2, axis=-1)
    cos, sin = jnp.cos(ang).astype(x.dtype), jnp.sin(ang).astype(x.dtype)
    return jnp.concatenate([x1 * cos - x2 * sin, x1 * sin + x2 * cos], axis=-1)


def axial_rope(x, ang_row, ang_col):
    extra = (1,) * (x.ndim - 3)
    ar = ang_row.reshape(ang_row.shape[0], *extra, ang_row.shape[-1])
    ac = ang_col.reshape(ang_col.shape[0], *extra, ang_col.shape[-1])
    xr, xc = jnp.split(x, 2, axis=-1)
    return jnp.concatenate([_rot(xr, ar), _rot(xc, ac)], axis=-1)


def mla_project(cq, ckv, krope, q_norm, w_uq, kv_norm, w_ukv, angles):
    B, T, _ = cq.shape
    q = (rms_norm(cq, q_norm) @ w_uq).reshape(B, T, MLA_HEADS, MLA_NOPE + MLA_ROPE)
    kv = (rms_norm(ckv, kv_norm) @ w_ukv).reshape(B, T, MLA_HEADS, MLA_NOPE + MLA_V)
    q_nope, q_rope = q[..., :MLA_NOPE], q[..., MLA_NOPE:]
    k_nope, v = kv[..., :MLA_NOPE], kv[..., MLA_NOPE:]
    if angles is not None:
        q_rope = axial_rope(q_rope, *angles)
        krope = axial_rope(krope, *angles)
    k_rope = jnp.broadcast_to(krope[:, :, None, :], (B, T, MLA_HEADS, MLA_ROPE))
    return (jnp.concatenate([q_nope, q_rope], -1), jnp.concatenate([k_nope, k_rope], -1), v)


def attention(q, k, v):
    s = jnp.einsum('bqhd,bkhd->bhqk', q, k, preferred_element_type=F32) * (MLA_NOPE + MLA_ROPE) ** -0.5
    p = jax.nn.softmax(s, axis=-1).astype(v.dtype)
    return jnp.einsum('bhqk,bkhd->bqhd', p, v)


def blocked_attention(q, k, v):
    B, T, H, D = q.shape
    nb = T // Q_BLOCK
    qb = q.reshape(B, nb, Q_BLOCK, H, D).swapaxes(0, 1)
    ob = lax.map(lambda qi: attention(qi, k, v), qb)
    return ob.swapaxes(0, 1).reshape(B, T, H * v.shape[-1])


def _pad_seq(u):
    return jnp.pad(u, ((0, 0), (1, 1), (0, 0)))


def short_conv(h, gate_b, gate_c, w):
    up = _pad_seq(gate_c * h)
    return gate_b * (up[:, :-2] * w[0] + up[:, 1:-1] * w[1] + up[:, 2:] * w[2])


def _heads(t):
    return t.reshape(*t.shape[:-1], RW_HEADS, RW_HEAD)


def rwkv_features(part, mu, w0, w_up, a0, a_up, g_up, k_k, k_a):
    B, T, _ = part.shape
    up = _pad_seq(part)
    part = part + (0.5 * (up[:, :-2] + up[:, 2:]) - part) * mu
    r, k, v, wd, ad, gd = _split(part, (RW_WIDTH, RW_WIDTH, RW_WIDTH, 2 * RW_DECAY_LORA, 2 * RW_ICLR_LORA, RW_GATE_LORA))
    wd = wd.reshape(B, T, 2, RW_DECAY_LORA)
    ad = ad.reshape(B, T, 2, RW_ICLR_LORA)
    w_log = -jax.nn.softplus(-(w0 + jnp.einsum('btzl,zlc->btzc', jnp.tanh(wd), w_up))) - 0.5
    decay = jnp.exp(-jnp.exp(w_log.astype(F32)))
    a = jax.nn.sigmoid(a0 + jnp.einsum('btzl,zlc->btzc', ad, a_up))
    g = jax.nn.sigmoid(gd) @ g_up
    kkf = _heads(k * k_k).astype(F32)
    kk = kkf * lax.rsqrt(jnp.maximum(jnp.sum(jnp.square(kkf), -1, keepdims=True), 1e-24))
    k_dir = k[:, :, None, :] * (1.0 + (a - 1.0) * k_a)
    return (r, k, v, g, kk, decay, a, k_dir)


def wkv_scan(state, r, w, k, v, kk, a, reverse):
    xs = tuple(jnp.moveaxis(t.astype(F32), 1, 0) for t in (r, w, k, v, kk, a))

    def step(S, inp):
        r_t, w_t, k_t, v_t, kk_t, a_t = inp
        s_kk = jnp.einsum('bhvk,bhk->bhv', S, kk_t)
        S = S * w_t[:, :, None, :] - s_kk[..., None] * (kk_t * a_t)[:, :, None, :] + v_t[..., None] * k_t[:, :, None, :]
        return S, jnp.einsum('bhvk,bhk->bhv', S, r_t)

    S, ys = lax.scan(step, state, xs, reverse=reverse)
    return S, jnp.moveaxis(ys, 0, 1)


def wkv_direction(state, feats, z, reverse):
    r, k, v, g, kk, decay, a, k_dir = feats
    return wkv_scan(state, _heads(r), _heads(decay[:, :, z]), _heads(k_dir[:, :, z]), _heads(v), kk,
                    _heads(a[:, :, z]), reverse)


def rwkv_readout(y, feats, r_k, gn_g, gn_b):
    r, k, v, g = feats[:4]
    B, T = r.shape[:2]
    mu = jnp.mean(y, -1, keepdims=True)
    var = jnp.mean(jnp.square(y - mu), -1, keepdims=True)
    yn = ((y - mu) * lax.rsqrt(var + RW_GN_EPS)).reshape(B, T, RW_WIDTH) * gn_g + gn_b
    bonus = jnp.sum(_heads(r * k * r_k).astype(F32), -1, keepdims=True) * _heads(v).astype(F32)
    return ((yn + bonus.reshape(B, T, RW_WIDTH)) * g).astype(r.dtype)


def gated_merge(gate_logits, br_a, br_b, br_c):
    gates = jax.nn.sigmoid(gate_logits).reshape(*gate_logits.shape[:-1], N_BRANCH, D_MODEL)
    return gates[..., 0, :] * br_a + gates[..., 1, :] * br_b + gates[..., 2, :] * br_c


def swiglu(h, w13, w2):
    u, gt = jnp.split(h @ w13, 2, axis=-1)
    return (jax.nn.silu(u) * gt) @ w2


def setup_inputs(seed: int = 0) -> dict:
    key = jax.random.key(seed)
    ks = iter(jax.random.split(key, 40))
    L, D = DEPTH, D_MODEL
    beta = (8.0 * DEPTH) ** -0.25

    def nrm(shape, scale):
        return jax.random.normal(next(ks), shape, F32) * scale

    return {
        "x": nrm((BATCH, SEQ, D), 1.0),
        "c": nrm((BATCH, D), 1.0),
        "ctx": nrm((BATCH, CTX_LEN, D), 1.0),
        "c_ctx": nrm((D,), 1.0),
        "mod_w": nrm((L, D, 6 * D), 0.5 * D ** -0.5),
        "mod_b": nrm((L, 6 * D), 0.02),
        "w_in": nrm((L, D, D_IN), D ** -0.5),
        "q_norm": 1.0 + nrm((L, MLA_Q_LORA), 0.02),
        "w_uq": nrm((L, MLA_Q_LORA, MLA_HEADS * (MLA_NOPE + MLA_ROPE)), MLA_Q_LORA ** -0.5),
        "kv_norm": 1.0 + nrm((L, MLA_KV_LORA), 0.02),
        "w_ukv": nrm((L, MLA_KV_LORA, MLA_HEADS * (MLA_NOPE + MLA_V)), MLA_KV_LORA ** -0.5),
        "w_o_attn": nrm((L, MLA_WIDTH, D), MLA_WIDTH ** -0.5),
        "conv_w": nrm((L, CONV_K, CONV_WIDTH), CONV_K ** -0.5),
        "w_o_conv": nrm((L, CONV_WIDTH, D), CONV_WIDTH ** -0.5),
        "rw_mu": jax.random.uniform(next(ks), (L, RW_IN), F32),
        "rw_w0": jax.random.uniform(next(ks), (L, 2, RW_WIDTH), F32, -6.0, 1.0),
        "rw_w_up": nrm((L, 2, RW_DECAY_LORA, RW_WIDTH), 0.1),
        "rw_a0": nrm((L, 2, RW_WIDTH), 0.5),
        "rw_a_up": nrm((L, 2, RW_ICLR_LORA, RW_WIDTH), RW_ICLR_LORA ** -0.5),
        "rw_g_up": nrm((L, RW_GATE_LORA, RW_WIDTH), RW_GATE_LORA ** -0.5),
        "rw_k_k": 0.85 + nrm((L, RW_WIDTH), 0.02),
        "rw_k_a": 1.0 + nrm((L, RW_WIDTH), 0.02),
        "rw_r_k": nrm((L, RW_WIDTH), 0.1),
        "rw_gn_g": 1.0 + nrm((L, RW_WIDTH), 0.02),
        "rw_gn_b": nrm((L, RW_WIDTH), 0.02),
        "w_o_rwkv": nrm((L, RW_WIDTH, D), RW_WIDTH ** -0.5),
        "w_out": nrm((L, D, D), beta * D ** -0.5),
        "ln1_g": 1.0 + nrm((L, D), 0.02),
        "ln1_b": nrm((L, D), 0.02),
        "ffn_w13": nrm((L, D, 2 * D_FF), D ** -0.5),
        "ffn_w2": nrm((L, D_FF, D), beta * D_FF ** -0.5),
        "ln2_g": 1.0 + nrm((L, D), 0.02),
        "ln2_b": nrm((L, D), 0.02),
    }


def reference(x, c, ctx, c_ctx, mod_w, mod_b, w_in, q_norm, w_uq, kv_norm, w_ukv, w_o_attn,
              conv_w, w_o_conv, rw_mu, rw_w0, rw_w_up, rw_a0, rw_a_up, rw_g_up, rw_k_k, rw_k_a,
              rw_r_k, rw_gn_g, rw_gn_b, w_o_rwkv, w_out, ln1_g, ln1_b, ffn_w13, ffn_w2, ln2_g, ln2_b):
    B, T, _ = x.shape
    angles = axial_rope_angles(T)
    alpha = (2.0 * DEPTH) ** 0.25
    s_lat = jax.nn.silu(c)
    s_ctx = jax.nn.silu(c_ctx)
    xc = ctx
    for l in range(DEPTH):
        last = l == DEPTH - 1
        sh1, sc1, g1, sh2, sc2, g2 = jnp.split((s_lat @ mod_w[l] + mod_b[l])[:, None, :], 6, axis=-1)
        csh1, csc1, cg1, csh2, csc2, cg2 = jnp.split(s_ctx @ mod_w[l] + mod_b[l], 6, axis=-1)

        cq_l, ckv_l, kr_l, ch_l, cb_l, cc_l, rw_l, gt_l = _split(modulate(x, sh1, sc1) @ w_in[l], IN_SPLITS)
        cq_c, ckv_c, kr_c, ch_c, cb_c, cc_c, rw_c, gt_c = _split(modulate(xc, csh1, csc1) @ w_in[l], IN_SPLITS)

        q_l, k_l, v_l = mla_project(cq_l, ckv_l, kr_l, q_norm[l], w_uq[l], kv_norm[l], w_ukv[l], angles)
        q_c, k_c, v_c = mla_project(cq_c, ckv_c, kr_c, q_norm[l], w_uq[l], kv_norm[l], w_ukv[l], None)
        att_l = blocked_attention(q_l, jnp.concatenate([k_c, k_l], 1), jnp.concatenate([v_c, v_l], 1))

        rw_par = (rw_mu[l], rw_w0[l], rw_w_up[l], rw_a0[l], rw_a_up[l], rw_g_up[l], rw_k_k[l], rw_k_a[l])
        feats_c = rwkv_features(rw_c, *rw_par)
        feats_l = rwkv_features(rw_l, *rw_par)
        zero_state = jnp.zeros((B, RW_HEADS, RW_HEAD, RW_HEAD), F32)
        sf_c, yf_c = wkv_direction(zero_state, feats_c, 0, False)
        sb_c, yb_c = wkv_direction(zero_state, feats_c, 1, True)
        _, yf_l = wkv_direction(sf_c, feats_l, 0, False)
        _, yb_l = wkv_direction(sb_c, feats_l, 1, True)
        rwo_l = rwkv_readout(yf_l + yb_l, feats_l, rw_r_k[l], rw_gn_g[l], rw_gn_b[l])

        o_l = gated_merge(gt_l, att_l @ w_o_attn[l],
                          short_conv(ch_l, cb_l, cc_l, conv_w[l]) @ w_o_conv[l],
                          rwo_l @ w_o_rwkv[l]) @ w_out[l]
        x_new = layer_norm(alpha * x + g1 * o_l, ln1_g[l], ln1_b[l])
        x_new = layer_norm(alpha * x_new + g2 * swiglu(modulate(x_new, sh2, sc2), ffn_w13[l], ffn_w2[l]),
                           ln2_g[l], ln2_b[l])

        if not last:
            att_c = attention(q_c, k_c, v_c).reshape(B, xc.shape[1], MLA_WIDTH)
            rwo_c = rwkv_readout(yf_c + yb_c, feats_c, rw_r_k[l], rw_gn_g[l], rw_gn_b[l])
            o_c = gated_merge(gt_c, att_c @ w_o_attn[l],
                              short_conv(ch_c, cb_c, cc_c, conv_w[l]) @ w_o_conv[l],
                              rwo_c @ w_o_rwkv[l]) @ w_out[l]
            xc_new = layer_norm(alpha * xc + cg1 * o_c, ln1_g[l], ln1_b[l])
            xc = layer_norm(alpha * xc_new + cg2 * swiglu(modulate(xc_new, csh2, csc2), ffn_w13[l], ffn_w2[l]),
                            ln2_g[l], ln2_b[l])
        x = x_new
    return x
```

```python
import numpy as np
from contextlib import ExitStack
import concourse.bass as bass
import concourse.mybir as mybir
from concourse.bass_utils import run_bass_kernel_spmd

ALU = mybir.AluOpType
AF = mybir.ActivationFunctionType
F32 = mybir.dt.float32
BF16 = mybir.dt.bfloat16
AX = mybir.AxisListType

D = 1024
DC = 8
TC = 256
NHEAD = 8
DFF = 2816
FC = 22
DIN = 7200
CDEC = float(np.exp(-0.5))
ALPHA = float(8.0 ** 0.25)
LN_EPS = 1e-5
RMS_EPS = 1e-6
GN_EPS = 64e-5
SCALE = float(96.0 ** -0.5)
CH = 64

PV_MODB = 0
PV_LN1G = 48
PV_LN1B = 56
PV_LN2G = 64
PV_LN2B = 72
PV_QN = 80
PV_KVN = 83
PV_CONV = 85
PV_MU = 97
PV_W0 = 112
PV_A0 = 120
PV_KK = 128
PV_KA = 132
PV_RK = 136
PV_GNG = 140
PV_GNB = 144
NPV = 148

CO_ID = 0
CO_ONES = 128
CO_BONES = 256
CO_MSL = 384
CO_MSU = 448
CO_MIL = 512
CO_MIU = 576
CO_ID64 = 640
NCONST = 704


class Res:
    def __init__(self, t=None):
        self.t = t
        self.w = None
        self.r = {}
        self.dsem = None
        self.dcnt = 0
        self.excl = False

    def __getitem__(self, k):
        return self.t[k]


class Eng:
    def __init__(self, e):
        self.e = e
        self.sems = []
        self.cnt = 0
        self.waited = {}


class KB:
    def __init__(self, nc, es):
        self.nc = nc
        self.es = es
        self.PE = Eng(nc.tensor)
        self.ACT = Eng(nc.scalar)
        self.DVE = Eng(nc.vector)
        self.POOL = Eng(nc.gpsimd)
        self.SP = Eng(nc.sync)
        self.nsem = 0
        self.free_dsems = []
        self.semobjs = {}

    def newsem(self):
        s = self.es.enter_context(self.nc.semaphore("s%d" % self.nsem))
        self.semobjs[self.nsem] = s
        self.nsem += 1
        return self.nsem - 1

    def sb(self, name, shape, dt):
        t = self.es.enter_context(self.nc.sbuf_tensor("sb_" + name, shape, dt))
        return Res(t)

    def ps(self, name, shape, dt=F32):
        t = self.es.enter_context(self.nc.psum_tensor("ps_" + name, shape, dt))
        r = Res(t)
        r.excl = True
        return r

    def dram(self, name, shape, dt):
        t = self.nc.dram_tensor("dr_" + name, shape, dt, kind="Internal")
        return Res(t.ap())

    def _wait(self, E, outs, ins, waw=True):
        deps = {}

        def add(tok):
            if tok is None:
                return
            s, v = tok
            if deps.get(s, 0) < v:
                deps[s] = v
        for t in ins:
            add(t.w)
            if t.excl:
                for s, v in t.r.items():
                    add((s, v))
        for t in outs:
            if waw:
                add(t.w)
            for s, v in t.r.items():
                add((s, v))
        for s, v in deps.items():
            if E.waited.get(s, 0) < v:
                E.e.wait_ge(self.semobjs[s], v)
                E.waited[s] = v

    def op(self, E, fn, outs=(), ins=()):
        self._wait(E, outs, ins)
        inst = fn(E.e)
        if E.cnt % 20000 == 0:
            E.sems.append(self.newsem())
            E.base = E.cnt
        E.cnt += 1
        s = E.sems[-1]
        v = E.cnt - E.base
        inst.then_inc(self.semobjs[s], 1)
        tok = (s, v)
        for t in outs:
            t.w = tok
            t.r = {}
        for t in ins:
            if t.r.get(s, 0) < v:
                t.r[s] = v
        return tok

    def dma(self, Q, dst, pairs, ins=(), waw=True):
        self._wait(Q, (dst,), ins, waw=waw)
        if dst.dsem is None:
            if self.free_dsems:
                dst.dsem, dst.dcnt = self.free_dsems.pop()
            else:
                dst.dsem = self.newsem()
        for (o, i) in pairs:
            Q.e.dma_start(out=o, in_=i).then_inc(self.semobjs[dst.dsem], 16)
            dst.dcnt += 16
        tok = (dst.dsem, dst.dcnt)
        dst.w = tok
        dst.r = {}
        for t in ins:
            if t.r.get(tok[0], 0) < tok[1]:
                t.r[tok[0]] = tok[1]
        return tok

    def finish(self, resources):
        self._wait(self.SP, (), resources)


STOP = 99
STOPB = 0


class _Stop(Exception):
    pass


class Pool:
    def __init__(self, tiles):
        self.tiles = tiles
        self.i = 0

    def get(self):
        t = self.tiles[self.i % len(self.tiles)]
        self.i += 1
        return t


def build(TL, DEPTH, dbg=False):
    T = TC + TL
    NCH = T // CH
    segs = [(0, TC)] + [(TC + 512 * i, 512) for i in range(TL // 512)]
    if TL % 512:
        segs.append((TC + 512 * (TL // 512), TL % 512))
    NT128 = T // 128
    nc = bass.Bass("TRN2", target_bir_lowering=False)

    def din(name, shape):
        return nc.dram_tensor(name, list(shape), F32, kind="ExternalInput").ap()
    L = DEPTH
    x_in = din("x", [2, TL, D])
    ctx_in = din("ctx", [2, TC, D])
    cT_in = din("cT", [128, DC, 3])
    consts_in = din("consts", [128, NCONST])
    rope_in = din("rope", [128, 4, T])
    pvec_in = din("pvec", [L, 128, NPV])
    mod_w = din("mod_w", [L, D, 6 * D])
    w_in = din("w_in", [L, D, DIN])
    w_kr = din("w_kr", [L, D, 2 * 96])
    w_uq = din("w_uq", [L, 384, 768])
    w_uq_sw = din("w_uq_sw", [L, 384, 768])
    w_ukv = din("w_ukv", [L, 256, 1024])
    w_o_attn = din("w_o_attn", [L, 512, D])
    w_o_conv = din("w_o_conv", [L, 512, D])
    w_o_rwkv = din("w_o_rwkv", [L, 512, D])
    w_out = din("w_out", [L, D, D])
    ffn_w13 = din("ffn_w13", [L, D, 2 * DFF])
    ffn_w2 = din("ffn_w2", [L, DFF, D])
    rw_w_up = din("rw_w_up", [L, 128, 512])
    rw_a_up = din("rw_a_up", [L, 128, 512])
    rw_g_up = din("rw_g_up", [L, 128, 512])
    out_ap = nc.dram_tensor("out", [2, TL, D], F32, kind="ExternalOutput").ap()
    out_res = Res(out_ap)
    dbg_ap = nc.dram_tensor("dbg", [128, 4096], F32, kind="ExternalOutput").ap() if STOP != 99 else None
    dbg_res = Res(dbg_ap)

    es = ExitStack()
    with es:
        kb = KB(nc, es)
        PE, ACT, DVE, POOL, SP = kb.PE, kb.ACT, kb.DVE, kb.POOL, kb.SP
        NPROJ = 60
        PJ_CQ, PJ_CKV, PJ_CH, PJ_CB, PJ_CC, PJ_RW, PJ_KR, PJ_KRS = 0, 3, 5, 9, 13, 17, 32, 33
        xT = [kb.dram("xT%d" % b, [DC, 128, T], F32) for b in range(2)]
        hT_d = kb.dram("hT", [DC, 128, T], BF16)
        h2T_d = kb.dram("h2T", [DC, 128, T], BF16)
        projT = kb.dram("projT", [NPROJ, 128, T], BF16)
        brT = kb.dram("brT", [3, 4, 128, T], BF16)
        o2p = kb.dram("o2p", [DC, 128, T], F32)

        cst = kb.sb("cst", [128, NCONST], F32)
        kb.dma(SP, cst, [(cst[:], consts_in)])
        cstb = kb.sb("cstb", [128, NCONST], BF16)
        kb.op(DVE, lambda e: e.tensor_copy(out=cstb[:], in_=cst[:]), [cstb], [cst])
        rope = kb.sb("rope", [128, 4, T], BF16)
        kb.dma(POOL, rope, [(rope[:], rope_in)])
        ident_f = cst[:, CO_ID:CO_ID + 128]
        ones_f = cst[:, CO_ONES:CO_ONES + 128]
        ident_b = cstb[:, CO_ID:CO_ID + 128]
        ones_b = cstb[:, CO_ONES:CO_ONES + 128]
        bones_b = cstb[:, CO_BONES:CO_BONES + 128]
        sT = kb.sb("sT", [128, DC, 3], F32)
        kb.dma(SP, sT, [(sT[:], cT_in)])
        kb.op(ACT, lambda e: e.activation(out=sT[:], in_=sT[:], func=AF.Silu), [sT], [sT])
        pv = kb.sb("pv", [128, NPV], F32)
        modv = kb.sb("modv", [128, 48, 3], F32)
        der = kb.sb("der", [128, 64], F32)
        DR_OMMU, DR_HMU, DR_OMKA = 0, 15, 30

        psp = Pool([kb.ps("ps%d" % i, [128, 512], F32) for i in range(6)])
        psacc = Pool([kb.ps("psacc%d" % i, [128, 512], F32) for i in range(2)])
        wbp = Pool([kb.sb("wb%d" % i, [128, 8 * 512], BF16) for i in range(3)])

        ARENA = 156 * 1024
        arena_t = es.enter_context(nc.sbuf_tensor("arena", [128, ARENA // 2], BF16))

        class Carver:
            def __init__(self):
                self.off = 0
                self.live = []
                self.summary = {}

            def _merge(self, rs):
                for pr in rs:
                    if pr.dsem is not None:
                        kb.free_dsems.append((pr.dsem, pr.dcnt))
                        pr.dsem = None
                    if pr.w is not None:
                        self.summary[pr.w[0]] = max(self.summary.get(pr.w[0], 0), pr.w[1])
                    for s_, v_ in pr.r.items():
                        self.summary[s_] = max(self.summary.get(s_, 0), v_)

            def reset(self):
                self._merge(self.live)
                self.live = []
                self.off = 0

            def mark(self):
                return (self.off, len(self.live))

            def release(self, m):
                self._merge(self.live[m[1]:])
                self.live = self.live[:m[1]]
                self.off = m[0]

            def get(self, shape, dt):
                n = int(np.prod(shape[1:]))
                nb = n * (4 if dt == F32 else 2)
                nb = (nb + 63) // 64 * 64
                assert self.off + nb <= ARENA, (self.off, nb, shape)
                base = arena_t[:, self.off // 2:(self.off + nb) // 2]
                if dt == F32:
                    base = base.bitcast(F32)
                ap = base[:, 0:n]
                if len(shape) == 3:
                    ap = ap.rearrange("p (a b) -> p a b", a=shape[1])
                elif len(shape) == 4:
                    ap = ap.rearrange("p (a b c) -> p a b c", a=shape[1], b=shape[2])
                r = Res(ap)
                r.r = dict(self.summary)
                self.off += nb
                self.live.append(r)
                return r
        carv = Carver()

        def phase_begin():
            carv.reset()

        def seg_col(b, si):
            return 2 if si == 0 else b

        phase_begin()
        xin_p = Pool([carv.get([128, D], F32) for _ in range(2)])
        xo_p = Pool([carv.get([128, DC, 128], F32) for _ in range(2)])
        for b in range(2):
            for ti in range(NT128):
                xt = xin_p.get()
                src = ctx_in[b, ti * 128:(ti + 1) * 128, :] if ti < 2 else x_in[b, (ti - 2) * 128:(ti - 1) * 128, :]
                kb.dma(SP, xt, [(xt[:], src)])
                xo = xo_p.get()
                for half in range(2):
                    pt = psp.get()

                    def f(e, pt=pt, xt=xt, half=half):
                        for q in range(4):
                            dc = half * 4 + q
                            i = e.matmul(pt[:, q * 128:(q + 1) * 128], lhsT=xt[:, dc * 128:(dc + 1) * 128],
                                         rhs=ident_f, start=True, stop=True)
                        return i
                    kb.op(PE, f, [pt], [xt, cst])
                    kb.op(ACT if half else DVE,
                          lambda e, pt=pt, xo=xo, half=half: (e.activation(out=xo[:, half * 4:half * 4 + 4, :], in_=pt[:].rearrange("p (a b) -> p a b", a=4), func=AF.Copy)
                                                              if half else e.tensor_copy(out=xo[:, half * 4:half * 4 + 4, :], in_=pt[:].rearrange("p (a b) -> p a b", a=4))),
                          [xo], [pt])
                kb.dma(SP, xT[b], [(xT[b][:, :, ti * 128:(ti + 1) * 128].rearrange("c p t -> p c t"), xo[:])], ins=[xo], waw=False)

        def load_w(dst, view, src_ap):
            kb.dma(POOL, dst, [(view, src_ap)])

        def gemm_fm(wsrc, KC, groups, act, evac):
            for grp in groups:
                c0 = grp[0][1]
                c1 = grp[-1][1] + grp[-1][2]
                wb = wbp.get()
                wv = wb[:, 0:KC * (c1 - c0)].rearrange("p (k n) -> p k n", k=KC)
                load_w(wb, wv, wsrc[:, c0:c1].rearrange("(k p) n -> p k n", p=128))
                for si, (t0, n) in enumerate(segs):
                    for (cid, col0, ncol) in grp:
                        pt = psp.get()

                        def f(e, pt=pt, wv=wv, col0=col0, ncol=ncol, t0=t0, n=n):
                            for kc in range(KC):
                                i = e.matmul(pt[0:ncol, 0:n], lhsT=wv[:, kc, col0 - c0:col0 - c0 + ncol],
                                             rhs=act[:, kc, t0:t0 + n], start=(kc == 0), stop=(kc == KC - 1))
                            return i
                        kb.op(PE, f, [pt], [wb, act])
                        evac(cid, si, pt, ncol, n, t0)

        evac_rr = [0]

        def evac_engine():
            evac_rr[0] += 1
            return ACT if evac_rr[0] % 2 else DVE

        def copy_op(E, out, in_):
            if E is ACT:
                return lambda e: e.activation(out=out, in_=in_, func=AF.Copy)
            return lambda e: e.tensor_copy(out=out, in_=in_)

        def ln_block(y, n, gcol, bcol, outx, tmp_sq, stat):
            kb.op(ACT, lambda e: e.activation(out=tmp_sq[:, :, 0:n], in_=y[:, :, 0:n], func=AF.Square), [tmp_sq], [y])
            pm = psp.get()

            def f(e):
                for dc in range(DC):
                    i = e.matmul(pm[:, 0:n], lhsT=ones_f, rhs=y[:, dc, 0:n], start=(dc == 0), stop=(dc == DC - 1))
                return i
            kb.op(PE, f, [pm], [y, cst])
            pq = psp.get()

            def f2(e):
                for dc in range(DC):
                    i = e.matmul(pq[:, 0:n], lhsT=ones_f, rhs=tmp_sq[:, dc, 0:n], start=(dc == 0), stop=(dc == DC - 1))
                return i
            kb.op(PE, f2, [pq], [tmp_sq, cst])
            mean = stat[:, 0, 0:n]
            msq = stat[:, 1, 0:n]
            var = stat[:, 2, 0:n]
            rstd = stat[:, 3, 0:n]
            kb.op(ACT, lambda e: e.activation(out=mean, in_=pm[:, 0:n], func=AF.Copy, scale=1.0 / D), [stat], [pm])
            kb.op(DVE, lambda e: e.tensor_tensor(out=msq, in0=mean, in1=mean, op=ALU.mult), [stat], [stat])
            kb.op(DVE, lambda e: e.scalar_tensor_tensor(out=var, in0=pq[:, 0:n], scalar=1.0 / D, in1=msq, op0=ALU.mult, op1=ALU.subtract), [stat], [pq, stat])
            kb.op(DVE, lambda e: e.tensor_scalar(out=var, in0=var, scalar1=LN_EPS / (ALPHA * ALPHA), scalar2=None, op0=ALU.add), [stat], [stat])
            kb.op(ACT, lambda e: e.activation(out=msq, in_=var, func=AF.Sqrt), [stat], [stat])
            kb.op(DVE, lambda e: e.reciprocal(out=rstd, in_=msq), [stat], [stat])
            kb.op(DVE, lambda e: e.tensor_tensor(out=msq, in0=var, in1=rstd, op=ALU.mult), [stat], [stat])
            kb.op(DVE, lambda e: e.tensor_tensor(out=msq, in0=msq, in1=rstd, op=ALU.mult), [stat], [stat])
            kb.op(DVE, lambda e: e.tensor_scalar(out=msq, in0=msq, scalar1=-0.5, scalar2=1.5, op0=ALU.mult, op1=ALU.add), [stat], [stat])
            kb.op(DVE, lambda e: e.tensor_tensor(out=rstd, in0=rstd, in1=msq, op=ALU.mult), [stat], [stat])
            kb.op(DVE, lambda e: e.tensor_tensor(out=y[:, :, 0:n], in0=y[:, :, 0:n], in1=mean.unsqueeze(1).to_broadcast([128, DC, n]), op=ALU.subtract), [y], [y, stat])
            kb.op(POOL, lambda e: e.tensor_tensor(out=y[:, :, 0:n], in0=y[:, :, 0:n], in1=rstd.unsqueeze(1).to_broadcast([128, DC, n]), op=ALU.mult), [y], [y, stat])
            for dc in range(DC):
                E = DVE if dc % 2 else POOL
                kb.op(E, lambda e, dc=dc: e.tensor_scalar(out=outx[:, dc, 0:n], in0=y[:, dc, 0:n], scalar1=pv[:, gcol + dc:gcol + dc + 1],
                                                            scalar2=pv[:, bcol + dc:bcol + dc + 1], op0=ALU.mult, op1=ALU.add), [outx], [y, pv])

        def layers():
          for l in range(L):
            if STOP == 0:
                raise _Stop()
            kb.dma(SP, pv, [(pv[:], pvec_in[l])])
            phase_begin()
            mw_p = Pool([carv.get([128, DC, 512], F32) for _ in range(2)])
            for g in range(12):
                mw = mw_p.get()
                kb.dma(SP, mw, [(mw[:], mod_w[l][:, g * 512:(g + 1) * 512].rearrange("(k p) n -> p k n", p=128))])
                pt = psp.get()

                def f(e, mw=mw, pt=pt):
                    for q in range(4):
                        for kc in range(DC):
                            i = e.matmul(pt[:, q * 4:q * 4 + 3], lhsT=mw[:, kc, q * 128:(q + 1) * 128], rhs=sT[:, kc, :],
                                         start=(kc == 0), stop=(kc == DC - 1))
                    return i
                kb.op(PE, f, [pt], [mw, sT])
                for q in range(4):
                    c = g * 4 + q
                    kb.op(DVE, lambda e, pt=pt, q=q, c=c: e.tensor_scalar(out=modv[:, c, :], in0=pt[:, q * 4:q * 4 + 3], scalar1=pv[:, PV_MODB + c:PV_MODB + c + 1],
                                                                         scalar2=None, op0=ALU.add), [modv], [pt, pv])
            kb.op(DVE, lambda e: e.tensor_scalar(out=modv[:, 8:16, :], in0=modv[:, 8:16, :], scalar1=1.0, scalar2=None, op0=ALU.add), [modv], [modv])
            kb.op(DVE, lambda e: e.tensor_scalar(out=modv[:, 32:40, :], in0=modv[:, 32:40, :], scalar1=1.0, scalar2=None, op0=ALU.add), [modv], [modv])
            kb.op(DVE, lambda e: e.tensor_scalar(out=modv[:, 16:24, :], in0=modv[:, 16:24, :], scalar1=1.0 / ALPHA, scalar2=None, op0=ALU.mult), [modv], [modv])
            kb.op(DVE, lambda e: e.tensor_scalar(out=modv[:, 40:48, :], in0=modv[:, 40:48, :], scalar1=1.0 / ALPHA, scalar2=None, op0=ALU.mult), [modv], [modv])
            kb.op(DVE, lambda e: e.tensor_scalar(out=der[:, DR_OMMU:DR_OMMU + 15], in0=pv[:, PV_MU:PV_MU + 15], scalar1=-1.0, scalar2=1.0, op0=ALU.mult, op1=ALU.add), [der], [pv])
            kb.op(DVE, lambda e: e.tensor_scalar(out=der[:, DR_HMU:DR_HMU + 15], in0=pv[:, PV_MU:PV_MU + 15], scalar1=0.5, scalar2=None, op0=ALU.mult), [der], [pv])
            kb.op(DVE, lambda e: e.tensor_scalar(out=der[:, DR_OMKA:DR_OMKA + 4], in0=pv[:, PV_KA:PV_KA + 4], scalar1=-1.0, scalar2=1.0, op0=ALU.mult, op1=ALU.add), [der], [pv])

            if STOP == 1:
                raise _Stop()
            for b in range(2):
                phase_begin()
                hT = carv.get([128, DC, T], BF16)
                xs_p = Pool([carv.get([128, DC, 512], F32) for _ in range(2)])
                for si, (t0, n) in enumerate(segs):
                    xs = xs_p.get()
                    kb.dma(SP, xs, [(xs[:, :, 0:n], xT[b][:, :, t0:t0 + n].rearrange("c p t -> p c t"))], ins=[xT[b]])
                    col = seg_col(b, si)
                    for dc in range(DC):
                        E = DVE if dc % 2 else POOL
                        kb.op(E, lambda e, dc=dc, xs=xs, t0=t0, n=n, col=col: e.tensor_scalar(
                            out=hT[:, dc, t0:t0 + n], in0=xs[:, dc, 0:n], scalar1=modv[:, 8 + dc, col:col + 1],
                            scalar2=modv[:, dc, col:col + 1], op0=ALU.mult, op1=ALU.add), [hT], [xs, modv])
                kb.dma(SP, hT_d, [(hT_d[:].rearrange("c p t -> p c t"), hT[:])], ins=[hT])
                stg_p = Pool([carv.get([128, 512], BF16) for _ in range(6)])

                def evac_proj(cid, si, pt, ncol, n, t0):
                    st = stg_p.get()
                    E = evac_engine()
                    kb.op(E, copy_op(E, st[0:ncol, 0:n], pt[0:ncol, 0:n]), [st], [pt])
                    kb.dma(SP, projT, [(projT[cid, 0:ncol, t0:t0 + n], st[0:ncol, 0:n])], ins=[st], waw=False)
                groups = []
                groups.append([(PJ_CQ + i, 128 * i, 128) for i in range(3)])
                groups.append([(PJ_CKV + i, 384 + 128 * i, 128) for i in range(2)])
                for base, pj in ((672, PJ_CH), (1184, PJ_CB), (1696, PJ_CC)):
                    groups.append([(pj + i, base + 128 * i, 128) for i in range(4)])
                for g0 in range(0, 15, 4):
                    groups.append([(PJ_RW + i, 2208 + 128 * i, 128) for i in range(g0, min(g0 + 4, 15))])
                gemm_fm(w_in[l], DC, groups, hT, evac_proj)
                gemm_fm(w_kr[l], DC, [[(PJ_KR, 0, 96), (PJ_KRS, 96, 96)]], hT, evac_proj)

                if STOP == 2 and b == STOPB:
                    raise _Stop()
                phase_begin()
                TP = T + 4
                cin_p = Pool([carv.get([128, 3, T], BF16) for _ in range(2)])
                upad = carv.get([128, TP], F32)
                acc = carv.get([128, TP], F32)
                yc_p = Pool([carv.get([128, T], BF16) for _ in range(2)])
                kb.op(POOL, lambda e: e.memset(upad[:], 0.0), [upad], [])
                for j in range(4):
                    ci = cin_p.get()
                    kb.dma(SP, ci, [(ci[:, 0, :], projT[PJ_CH + j]), (ci[:, 1, :], projT[PJ_CB + j]), (ci[:, 2, :], projT[PJ_CC + j])], ins=[projT])
                    kb.op(DVE, lambda e, ci=ci: e.tensor_tensor(out=upad[:, 1:1 + TC], in0=ci[:, 0, 0:TC], in1=ci[:, 2, 0:TC], op=ALU.mult), [upad], [ci])
                    kb.op(DVE, lambda e, ci=ci: e.tensor_tensor(out=upad[:, 3 + TC:3 + T], in0=ci[:, 0, TC:T], in1=ci[:, 2, TC:T], op=ALU.mult), [upad], [ci])
                    w0c = pv[:, PV_CONV + j * 3 + 0:PV_CONV + j * 3 + 1]
                    w1c = pv[:, PV_CONV + j * 3 + 1:PV_CONV + j * 3 + 2]
                    w2c = pv[:, PV_CONV + j * 3 + 2:PV_CONV + j * 3 + 3]
                    kb.op(POOL, lambda e, w1c=w1c: e.tensor_scalar(out=acc[:, 1:TP - 1], in0=upad[:, 1:TP - 1], scalar1=w1c, scalar2=None, op0=ALU.mult), [acc], [upad, pv])
                    kb.op(DVE, lambda e, w0c=w0c: e.scalar_tensor_tensor(out=acc[:, 1:TP - 1], in0=upad[:, 0:TP - 2], scalar=w0c, in1=acc[:, 1:TP - 1], op0=ALU.mult, op1=ALU.add), [acc], [upad, pv, acc])
                    kb.op(DVE, lambda e, w2c=w2c: e.scalar_tensor_tensor(out=acc[:, 1:TP - 1], in0=upad[:, 2:TP], scalar=w2c, in1=acc[:, 1:TP - 1], op0=ALU.mult, op1=ALU.add), [acc], [upad, pv, acc])
                    yc = yc_p.get()
                    kb.op(DVE, lambda e, ci=ci, yc=yc: e.tensor_tensor(out=yc[:, 0:TC], in0=acc[:, 1:1 + TC], in1=ci[:, 1, 0:TC], op=ALU.mult), [yc], [acc, ci])
                    kb.op(POOL, lambda e, ci=ci, yc=yc: e.tensor_tensor(out=yc[:, TC:T], in0=acc[:, 3 + TC:3 + T], in1=ci[:, 1, TC:T], op=ALU.mult), [yc], [acc, ci])
                    kb.dma(SP, brT, [(brT[1, j], yc[:])], ins=[yc], waw=False)

                if STOP == 3 and b == STOPB:
                    raise _Stop()
                phase_begin()
                cqkv = carv.get([128, 5, T], BF16)
                sq = carv.get([128, 5, 512], BF16)
                rinv = carv.get([128, 2, 512], F32)
                wq = carv.get([128, 3, 768], BF16)
                wqs = carv.get([128, 3, 768], BF16)
                wkv = carv.get([128, 2, 1024], BF16)
                krot = carv.get([128, T], BF16)
                kT = carv.get([128, NHEAD, T], BF16)
                mk_ = carv.mark()
                krr = carv.get([128, 2, T], BF16)
                kt1 = carv.get([128, T], F32)
                wq_f = carv.get([128, 3, 768], F32)
                wkv_f = carv.get([128, 2, 1024], F32)
                kb.dma(SP, cqkv, [(cqkv[:], projT[PJ_CQ:PJ_CQ + 5].rearrange("c p t -> p c t"))], ins=[projT])
                kb.dma(SP, krr, [(krr[64:96, :, :], projT[PJ_KR:PJ_KR + 2, 64:96, :].rearrange("c p t -> p c t"))], ins=[projT])
                for si, (t0, n) in enumerate(segs):
                    kb.op(POOL, lambda e, t0=t0, n=n: e.tensor_tensor(out=sq[:, :, 0:n], in0=cqkv[:, :, t0:t0 + n], in1=cqkv[:, :, t0:t0 + n], op=ALU.mult), [sq], [cqkv])
                    for which, (c0, cn, dim) in enumerate(((0, 3, 384), (3, 2, 256))):
                        pt = psp.get()

                        def f(e, pt=pt, c0=c0, cn=cn, n=n):
                            for k in range(cn):
                                i = e.matmul(pt[:, 0:n], lhsT=ones_b, rhs=sq[:, c0 + k, 0:n], start=(k == 0), stop=(k == cn - 1))
                            return i
                        kb.op(PE, f, [pt], [sq, cstb])
                        kb.op(DVE, lambda e, pt=pt, n=n, dim=dim, which=which: e.tensor_scalar(out=rinv[:, which, 0:n], in0=pt[:, 0:n], scalar1=1.0 / dim, scalar2=RMS_EPS, op0=ALU.mult, op1=ALU.add), [rinv], [pt])
                        kb.op(ACT, lambda e, n=n, which=which: e.activation(out=rinv[:, which, 0:n], in_=rinv[:, which, 0:n], func=AF.Sqrt), [rinv], [rinv])
                        kb.op(DVE, lambda e, n=n, which=which: e.reciprocal(out=rinv[:, which, 0:n], in_=rinv[:, which, 0:n]), [rinv], [rinv])
                        kb.op(DVE, lambda e, t0=t0, n=n, c0=c0, cn=cn, which=which: e.tensor_tensor(
                            out=cqkv[:, c0:c0 + cn, t0:t0 + n], in0=cqkv[:, c0:c0 + cn, t0:t0 + n],
                            in1=rinv[:, which, 0:n].unsqueeze(1).to_broadcast([128, cn, n]), op=ALU.mult), [cqkv], [cqkv, rinv])
                for (src, dstw) in ((w_uq, wq), (w_uq_sw, wqs)):
                    kb.dma(SP, wq_f, [(wq_f[:], src[l].rearrange("(k p) n -> p k n", p=128))])
                    for k in range(3):
                        kb.op(DVE, lambda e, k=k, dstw=dstw: e.tensor_scalar(out=dstw[:, k, :], in0=wq_f[:, k, :], scalar1=pv[:, PV_QN + k:PV_QN + k + 1], scalar2=None, op0=ALU.mult), [dstw], [wq_f, pv])
                kb.dma(SP, wkv_f, [(wkv_f[:], w_ukv[l].rearrange("(k p) n -> p k n", p=128))])
                for k in range(2):
                    kb.op(DVE, lambda e, k=k: e.tensor_scalar(out=wkv[:, k, :], in0=wkv_f[:, k, :], scalar1=pv[:, PV_KVN + k:PV_KVN + k + 1], scalar2=None, op0=ALU.mult), [wkv], [wkv_f, pv])
                kb.op(DVE, lambda e: e.tensor_tensor(out=kt1[64:96, :], in0=krr[64:96, 0, :], in1=rope[64:96, 2, :], op=ALU.mult), [kt1], [krr, rope])
                kb.op(DVE, lambda e: e.tensor_tensor(out=krot[64:96, :], in0=krr[64:96, 1, :], in1=rope[64:96, 3, :], op=ALU.mult), [krot], [krr, rope])
                kb.op(DVE, lambda e: e.tensor_tensor(out=krot[64:96, :], in0=krot[64:96, :], in1=kt1[64:96, :], op=ALU.add), [krot], [krot, kt1])
                for h in range(NHEAD):
                    kb.op(POOL, lambda e, h=h: e.tensor_copy(out=kT[64:96, h, :], in_=krot[64:96, :]), [kT], [krot])
                    for si, (t0, n) in enumerate(segs):
                        pk = psp.get()

                        def f3(e, pk=pk, h=h, t0=t0, n=n):
                            for k in range(2):
                                i = e.matmul(pk[0:64, 0:n], lhsT=wkv[:, k, h * 128:h * 128 + 64], rhs=cqkv[:, 3 + k, t0:t0 + n], start=(k == 0), stop=(k == 1))
                            return i
                        kb.op(PE, f3, [pk], [wkv, cqkv])
                        E = evac_engine()
                        kb.op(E, copy_op(E, kT[0:64, h, t0:t0 + n], pk[0:64, 0:n]), [kT], [pk])
                carv.release(mk_)
                V = carv.get([128, NT128, NHEAD, 65], BF16)
                qs_p = Pool([carv.get([128, NHEAD, 512], BF16) for _ in range(2)])
                qtmp = carv.get([128, 2, 512], F32)
                kb.op(POOL, lambda e: e.memset(V[:], 1.0), [V], [])
                for kt in range(NT128):
                    pt = psp.get()

                    def f(e, pt=pt, kt=kt):
                        for k in range(2):
                            i = e.matmul(pt[:, 0:512].rearrange("p (h v) -> p h v", h=NHEAD), lhsT=cqkv[:, 3 + k, kt * 128:(kt + 1) * 128],
                                         rhs=wkv[:, k, :].rearrange("p (h x) -> p h x", h=NHEAD)[:, :, 64:128], start=(k == 0), stop=(k == 1))
                        return i
                    kb.op(PE, f, [pt], [cqkv, wkv])
                    E = evac_engine()
                    kb.op(E, copy_op(E, V[:, kt, :, 0:64], pt[:, 0:512].rearrange("p (h v) -> p h v", h=NHEAD)), [V], [pt])
                pT_p = Pool([carv.get([128, 512], BF16) for _ in range(3)])
                atm_p = Pool([carv.get([128, 4, 512], BF16) for _ in range(2)])
                rs_p = Pool([carv.get([128, 4], F32) for _ in range(2)])
                ats_p = Pool([carv.get([128, 4, 512], BF16) for _ in range(2)])
                for si, (t0, n) in enumerate(segs):
                    nq = n // 128
                    nkt = 2 if si == 0 else NT128
                    qs = qs_p.get()
                    for h in range(NHEAD):
                        pq = psp.get()
                        pqs = psp.get()

                        def f(e, pq=pq, h=h, t0=t0, n=n):
                            for k in range(3):
                                i = e.matmul(pq[0:96, 0:n], lhsT=wq[:, k, h * 96:(h + 1) * 96], rhs=cqkv[:, k, t0:t0 + n], start=(k == 0), stop=(k == 2))
                            return i
                        kb.op(PE, f, [pq], [wq, cqkv])

                        def f2(e, pqs=pqs, h=h, t0=t0, n=n):
                            for k in range(3):
                                i = e.matmul(pqs[0:96, 0:n], lhsT=wqs[:, k, h * 96:(h + 1) * 96], rhs=cqkv[:, k, t0:t0 + n], start=(k == 0), stop=(k == 2))
                            return i
                        kb.op(PE, f2, [pqs], [wqs, cqkv])
                        kb.op(ACT, lambda e, pq=pq, h=h, n=n, qs=qs: e.activation(out=qs[0:64, h, 0:n], in_=pq[0:64, 0:n], func=AF.Copy, scale=SCALE), [qs], [pq])
                        kb.op(DVE, lambda e, pq=pq, t0=t0, n=n: e.tensor_tensor(out=qtmp[64:96, 0, 0:n], in0=pq[64:96, 0:n], in1=rope[64:96, 0, t0:t0 + n], op=ALU.mult), [qtmp], [pq, rope])
                        kb.op(DVE, lambda e, pqs=pqs, t0=t0, n=n: e.tensor_tensor(out=qtmp[64:96, 1, 0:n], in0=pqs[64:96, 0:n], in1=rope[64:96, 1, t0:t0 + n], op=ALU.mult), [qtmp], [pqs, rope, qtmp])
                        kb.op(DVE, lambda e, h=h, n=n, qs=qs: e.tensor_tensor(out=qs[64:96, h, 0:n], in0=qtmp[64:96, 0, 0:n], in1=qtmp[64:96, 1, 0:n], op=ALU.add), [qs], [qtmp])
                    atm = atm_p.get()
                    for h in range(NHEAD):
                        po = psacc.get()
                        pov = po[:].rearrange("p (q x) -> p q x", q=4)
                        for kt in range(nkt):
                            pss = psp.get()
                            kb.op(PE, lambda e, pss=pss, h=h, kt=kt, n=n, qs=qs: e.matmul(pss[:, 0:n], lhsT=kT[0:96, h, kt * 128:(kt + 1) * 128], rhs=qs[0:96, h, 0:n], start=True, stop=True), [pss], [kT, qs])
                            pT = pT_p.get()
                            kb.op(ACT, lambda e, pss=pss, pT=pT, n=n: e.activation(out=pT[:, 0:n], in_=pss[:, 0:n], func=AF.Exp), [pT], [pss])

                            def f(e, pT=pT, kt=kt, h=h, nq=nq, nkt=nkt, pov=pov):
                                for qq in range(nq):
                                    i = e.matmul(pov[:, qq, 0:65], lhsT=pT[:, qq * 128:(qq + 1) * 128], rhs=V[:, kt, h, :], start=(kt == 0 and qq == 0), stop=(kt == nkt - 1), skip_group_check=True)
                                return i
                            kb.op(PE, f, [po], [pT, V])
                        rs = rs_p.get()
                        kb.op(DVE, lambda e, rs=rs, pov=pov, nq=nq: e.reciprocal(out=rs[:, 0:nq], in_=pov[:, 0:nq, 64]), [rs], [po])
                        for qq in range(nq):
                            kb.op(DVE, lambda e, rs=rs, pov=pov, qq=qq, h=h, atm=atm: e.tensor_scalar(out=atm[:, qq, h * 64:(h + 1) * 64], in0=pov[:, qq, 0:64], scalar1=rs[:, qq:qq + 1], scalar2=None, op0=ALU.mult), [atm], [po, rs])
                    ats = ats_p.get()
                    for qq in range(nq):
                        pt = psp.get()

                        def f(e, pt=pt, qq=qq, atm=atm):
                            for cj in range(4):
                                i = e.matmul(pt[:, cj * 128:(cj + 1) * 128], lhsT=atm[:, qq, cj * 128:(cj + 1) * 128], rhs=ident_b, start=True, stop=True)
                            return i
                        kb.op(PE, f, [pt], [atm, cstb])
                        E = evac_engine()
                        kb.op(E, copy_op(E, ats[:, :, qq * 128:(qq + 1) * 128], pt[:].rearrange("p (c t) -> p c t", c=4)), [ats], [pt])
                    kb.dma(SP, brT, [(brT[0, :, :, t0:t0 + n].rearrange("c p t -> p c t"), ats[:, :, 0:n])], ins=[ats], waw=False)

                if STOP == 4 and b == STOPB:
                    raise _Stop()
                phase_begin()
                lora = carv.get([128, 3, T], BF16)
                rkv = carv.get([128, 3, T], BF16)
                PD = T + 6
                CO0 = 2
                LO0 = 4 + TC
                rp_p = Pool([carv.get([128, PD], BF16) for _ in range(2)])
                f1 = carv.get([128, PD], F32)
                f2t = carv.get([128, PD], F32)
                for rp in rp_p.tiles:
                    kb.op(POOL, lambda e, rp=rp: e.memset(rp[:], 0.0), [rp], [])

                def shift_chunk(c, dst, dv, fn):
                    rp = rp_p.get()
                    kb.dma(SP, rp, [(rp[:, CO0:CO0 + TC], projT[PJ_RW + c, :, 0:TC]), (rp[:, LO0:LO0 + TL], projT[PJ_RW + c, :, TC:T])], ins=[projT])
                    kb.op(POOL, lambda e, rp=rp: e.tensor_tensor(out=f1[:, 1:PD - 1], in0=rp[:, 0:PD - 2], in1=rp[:, 2:PD], op=ALU.add), [f1], [rp])
                    kb.op(DVE, lambda e, rp=rp, c=c: e.tensor_scalar(out=f2t[:, 1:PD - 1], in0=rp[:, 1:PD - 1], scalar1=der[:, DR_OMMU + c:DR_OMMU + c + 1], scalar2=None, op0=ALU.mult), [f2t], [rp, der])
                    kb.op(DVE, lambda e, c=c: e.scalar_tensor_tensor(out=f2t[:, 1:PD - 1], in0=f1[:, 1:PD - 1], scalar=der[:, DR_HMU + c:DR_HMU + c + 1], in1=f2t[:, 1:PD - 1], op0=ALU.mult, op1=ALU.add), [f2t], [f1, der, f2t])
                    kb.op(ACT, lambda e, dv=dv, fn=fn: e.activation(out=dv[:, 0:TC], in_=f2t[:, CO0:CO0 + TC], func=fn), [dst], [f2t])
                    kb.op(ACT, lambda e, dv=dv, fn=fn: e.activation(out=dv[:, TC:T], in_=f2t[:, LO0:LO0 + TL], func=fn), [dst], [f2t])
                for ci_, fn_ in enumerate((AF.Tanh, AF.Copy, AF.Sigmoid)):
                    shift_chunk(12 + ci_, lora, lora[:, ci_, :], fn_)
                if STOP == 41:
                    raise _Stop()
                wup = carv.get([128, 512], BF16)
                aup = carv.get([128, 512], BF16)
                gup = carv.get([128, 512], BF16)
                load_w(wup, wup[:], rw_w_up[l])
                load_w(aup, aup[:], rw_a_up[l])
                load_w(gup, gup[:], rw_g_up[l])
                smask = carv.get([128, T], F32)
                kb.op(POOL, lambda e: e.memset(smask[:], 1.0), [smask], [])
                kb.op(POOL, lambda e: e.memset(smask[:].rearrange("p (c t) -> p c t", t=CH)[:, :, 0:1], 0.0), [smask], [smask])
                gT = carv.get([128, T], BF16)
                kk = carv.get([128, T], BF16)
                bonus = carv.get([128, T], BF16)
                f3t = carv.get([128, T], F32)
                lw = carv.get([128, T], F32)
                a_t = carv.get([128, T], BF16)
                tot = carv.get([128, NCH], F32)
                wc = carv.get([128, NCH], F32)
                AR = carv.get([128, 2, T], BF16)
                KhT = carv.get([128, T], BF16)
                nBT = carv.get([128, T], BF16)
                Vs = carv.get([128, NCH, CH], BF16)
                yacc = carv.get([128, NCH, CH], F32)
                yn = carv.get([128, NCH, CH], BF16)
                NB = 4
                As = carv.get([128, NB, CH], BF16)
                Ks = carv.get([128, NB, CH], BF16)
                nBs = carv.get([128, NB, CH], BF16)
                X1 = carv.get([128, NB, CH], BF16)
                Xc_p = Pool([carv.get([128, NB, CH], BF16) for _ in range(2)])
                XT_p = Pool([carv.get([128, NB, 2, CH], BF16) for _ in range(2)])
                G2s = carv.get([128, NB, 2, CH], BF16)
                G3s = carv.get([128, NB, 2, CH], BF16)
                LkVs = carv.get([128, NB, CH], BF16)
                PTs = carv.get([128, NB, CH], BF16)
                Us_p = Pool([carv.get([128, CH], BF16) for _ in range(2)])
                ST_p = Pool([carv.get([128, CH], BF16) for _ in range(2)])
                gst = carv.get([128, 4, NCH], F32)

                def qmm(e, out_fn, lhs_fn, rhs_fn, start=True, stop=True):
                    i = None
                    for h in range(2):
                        sl = slice(h * 64, (h + 1) * 64)
                        i = e.matmul(out_fn(sl), lhsT=lhs_fn(sl), rhs=rhs_fn(sl), start=start, stop=stop,
                                     tile_position=(h * 64, h * 64))
                    return i

                def mask(off, nb, two=False):
                    m = cst[:, off:off + CH]
                    return m.unsqueeze(1).to_broadcast([128, nb, CH])

                for j in range(4):
                    for idx_, c_ in enumerate((j, 4 + j, 8 + j)):
                        shift_chunk(c_, rkv, rkv[:, idx_, :], AF.Copy)
                    rj = rkv[:, 0, :]
                    kj = rkv[:, 1, :]
                    vj = rkv[:, 2, :]
                    for si, (t0, n) in enumerate(segs):
                        pt = psp.get()
                        kb.op(PE, lambda e, pt=pt, t0=t0, n=n, j=j: e.matmul(pt[:, 0:n], lhsT=gup[:, j * 128:(j + 1) * 128], rhs=lora[:, 2, t0:t0 + n], start=True, stop=True), [pt], [gup, lora])
                        kb.op(ACT, lambda e, pt=pt, t0=t0, n=n: e.activation(out=gT[:, t0:t0 + n], in_=pt[:, 0:n], func=AF.Copy), [gT], [pt])
                    kb.op(DVE, lambda e, kj=kj, j=j: e.tensor_scalar(out=f1[:, 0:T], in0=kj, scalar1=pv[:, PV_KK + j:PV_KK + j + 1], scalar2=None, op0=ALU.mult), [f1], [rkv, pv])
                    kb.op(POOL, lambda e: e.tensor_tensor(out=a_t[:], in0=f1[:, 0:T], in1=f1[:, 0:T], op=ALU.mult), [a_t], [f1])
                    kb.op(DVE, lambda e, rj=rj, kj=kj, j=j: e.scalar_tensor_tensor(out=bonus[:], in0=rj, scalar=pv[:, PV_RK + j:PV_RK + j + 1], in1=kj, op0=ALU.mult, op1=ALU.mult), [bonus], [rkv, pv])
                    for si, (t0, n) in enumerate(segs):
                        pt = psp.get()
                        kb.op(PE, lambda e, pt=pt, t0=t0, n=n: e.matmul(pt[:, 0:n], lhsT=bones_b, rhs=a_t[:, t0:t0 + n], start=True, stop=True), [pt], [a_t, cstb])
                        kb.op(DVE, lambda e, pt=pt, t0=t0, n=n: e.tensor_scalar(out=f2t[:, t0:t0 + n], in0=pt[:, 0:n], scalar1=1e-24, scalar2=None, op0=ALU.max), [f2t], [pt])
                        pt2 = psp.get()
                        kb.op(PE, lambda e, pt2=pt2, t0=t0, n=n: e.matmul(pt2[:, 0:n], lhsT=bones_b, rhs=bonus[:, t0:t0 + n], start=True, stop=True), [pt2], [bonus, cstb])
                        kb.op(DVE, lambda e, pt2=pt2, t0=t0, n=n, vj=vj: e.tensor_tensor(out=f3t[:, t0:t0 + n], in0=pt2[:, 0:n], in1=vj[:, t0:t0 + n], op=ALU.mult), [f3t], [pt2, rkv])
                    kb.op(ACT, lambda e: e.activation(out=f2t[:, 0:T], in_=f2t[:, 0:T], func=AF.Sqrt), [f2t], [f2t])
                    kb.op(DVE, lambda e: e.reciprocal(out=f2t[:, 0:T], in_=f2t[:, 0:T]), [f2t], [f2t])
                    kb.op(DVE, lambda e: e.tensor_tensor(out=kk[:], in0=f1[:, 0:T], in1=f2t[:, 0:T], op=ALU.mult), [kk], [f1, f2t])
                    kb.op(POOL, lambda e: e.tensor_copy(out=bonus[:], in_=f3t[:]), [bonus], [f3t])
                    for c0 in range(0, NCH, 8):
                        ncc = min(8, NCH - c0)
                        pt = psp.get()
                        ptv = pt[:].rearrange("p (c t) -> p c t", c=8)

                        def f(e, ptv=ptv, c0=c0, ncc=ncc, vj=vj):
                            for cc in range(ncc):
                                ch = c0 + cc
                                i = qmm(e, lambda sl: ptv[sl, cc, :], lambda sl: vj[sl, ch * CH:(ch + 1) * CH], lambda sl: ident_b[sl, sl])
                            return i
                        kb.op(PE, f, [pt], [rkv, cstb])
                        E = evac_engine()
                        kb.op(E, copy_op(E, Vs[:, c0:c0 + ncc, :], ptv[:, 0:ncc, :]), [Vs], [pt])

                    if STOP == 42:
                        raise _Stop()
                    for z in range(2):
                        zs = slice(z * 64, (z + 1) * 64)
                        for si, (t0, n) in enumerate(segs):
                            pt = psp.get()
                            kb.op(PE, lambda e, pt=pt, t0=t0, n=n, j=j, zs=zs: e.matmul(pt[:, 0:n], lhsT=wup[zs, j * 128:(j + 1) * 128], rhs=lora[zs, 0, t0:t0 + n], start=True, stop=True), [pt], [wup, lora])
                            kb.op(ACT, lambda e, pt=pt, t0=t0, n=n, j=j, z=z: e.activation(out=lw[:, t0:t0 + n], in_=pt[:, 0:n], func=AF.Sigmoid, bias=pv[:, PV_W0 + z * 4 + j:PV_W0 + z * 4 + j + 1]), [lw], [pt, pv])
                            pt2 = psp.get()
                            kb.op(PE, lambda e, pt2=pt2, t0=t0, n=n, j=j, zs=zs: e.matmul(pt2[:, 0:n], lhsT=aup[zs, j * 128:(j + 1) * 128], rhs=lora[zs, 1, t0:t0 + n], start=True, stop=True), [pt2], [aup, lora])
                            kb.op(ACT, lambda e, pt2=pt2, t0=t0, n=n, j=j, z=z: e.activation(out=a_t[:, t0:t0 + n], in_=pt2[:, 0:n], func=AF.Sigmoid, bias=pv[:, PV_A0 + z * 4 + j:PV_A0 + z * 4 + j + 1]), [a_t], [pt2, pv])
                        kb.op(DVE, lambda e: e.tensor_tensor_scan(out=f1[:, 0:T], data0=smask[:], data1=lw[:], initial=0.0, op0=ALU.mult, op1=ALU.add), [f1], [smask, lw])
                        f1v = f1[:, 0:T].rearrange("p (c t) -> p c t", t=CH)
                        kb.op(POOL, lambda e, f1v=f1v: e.tensor_copy(out=tot[:], in_=f1v[:, :, CH - 1]), [tot], [f1])
                        kb.op(ACT, lambda e: e.activation(out=wc[:], in_=tot[:], func=AF.Exp, scale=-CDEC), [wc], [tot])
                        if z == 0:
                            kb.op(DVE, lambda e: e.tensor_tensor(out=f2t[:, 0:T], in0=f1[:, 0:T], in1=lw[:], op=ALU.subtract), [f2t], [f1, lw])
                            CI, CE = f1, f2t
                        else:
                            kb.op(DVE, lambda e, f1v=f1v: e.tensor_tensor(out=f2t[:, 0:T].rearrange("p (c t) -> p c t", t=CH), in0=tot[:].unsqueeze(2).to_broadcast([128, NCH, CH]), in1=f1v, op=ALU.subtract), [f2t], [f1, tot])
                            kb.op(POOL, lambda e: e.tensor_tensor(out=f3t[:], in0=f2t[:, 0:T], in1=lw[:], op=ALU.add), [f3t], [f2t, lw])
                            CI, CE = f3t, f2t
                        kb.op(ACT, lambda e, CE=CE: e.activation(out=CE[:, 0:T], in_=CE[:, 0:T], func=AF.Exp, scale=-CDEC), [CE], [CE])
                        kb.op(DVE, lambda e, CE=CE: e.tensor_tensor(out=AR[:, 0, :], in0=kk[:], in1=CE[:, 0:T], op=ALU.mult), [AR], [kk, CE])
                        kb.op(ACT, lambda e, CI=CI: e.activation(out=lw[:], in_=CI[:, 0:T], func=AF.Exp, scale=-CDEC), [lw], [CI])
                        kb.op(POOL, lambda e, rj=rj: e.tensor_tensor(out=AR[:, 1, :], in0=rj, in1=lw[:], op=ALU.mult), [AR], [rkv, lw])
                        kb.op(ACT, lambda e, CI=CI: e.activation(out=CI[:, 0:T], in_=CI[:, 0:T], func=AF.Exp, scale=CDEC), [CI], [CI])
                        kb.op(DVE, lambda e, j=j: e.tensor_scalar(out=lw[:], in0=a_t[:], scalar1=pv[:, PV_KA + j:PV_KA + j + 1], scalar2=der[:, DR_OMKA + j:DR_OMKA + j + 1], op0=ALU.mult, op1=ALU.add), [lw], [a_t, pv, der])
                        kb.op(POOL, lambda e, kj=kj: e.tensor_tensor(out=lw[:], in0=lw[:], in1=kj, op=ALU.mult), [lw], [lw, rkv])
                        kb.op(DVE, lambda e, CI=CI: e.tensor_tensor(out=KhT[:], in0=lw[:], in1=CI[:, 0:T], op=ALU.mult), [KhT], [lw, CI])
                        kb.op(POOL, lambda e: e.tensor_tensor(out=lw[:], in0=kk[:], in1=a_t[:], op=ALU.mult), [lw], [kk, a_t])
                        kb.op(DVE, lambda e, CI=CI: e.scalar_tensor_tensor(out=nBT[:], in0=lw[:], scalar=-1.0, in1=CI[:, 0:T], op0=ALU.mult, op1=ALU.mult), [nBT], [lw, CI])

                        if STOP == 43:
                            raise _Stop()
                        if z == 0:
                            m_x1, m_T, m_Ti = CO_MSL, CO_MSU, CO_MIU
                        else:
                            m_x1, m_T, m_Ti = CO_MSU, CO_MSL, CO_MIL
                        if z == 0:
                            order = list(range(NCH))
                        else:
                            order = [3, 2, 1, 0] + list(range(NCH - 1, 3, -1))
                        ST = ST_p.get()
                        kb.op(DVE, lambda e, ST=ST: e.tensor_scalar(out=ST[:], in0=cstb[:, 0:CH], scalar1=0.0, scalar2=None, op0=ALU.mult), [ST], [cstb])
                        for s0 in range(0, NCH, NB):
                            chs = order[s0:s0 + NB]
                            nb = len(chs)
                            if STOP == 4300:
                                raise _Stop()
                            pt = psp.get()
                            ptv = pt[:].rearrange("p (c t) -> p c t", c=8)

                            def f(e, ptv=ptv, chs=chs):
                                for ci, ch in enumerate(chs):
                                    cs = slice(ch * CH, (ch + 1) * CH)
                                    qmm(e, lambda sl: ptv[sl, ci, :], lambda sl: AR[sl, 0, cs], lambda sl: ident_b[sl, sl])
                                    i = qmm(e, lambda sl: ptv[sl, 4 + ci, :], lambda sl: KhT[sl, cs], lambda sl: ident_b[sl, sl])
                                return i
                            kb.op(PE, f, [pt], [AR, KhT, cstb])
                            kb.op(ACT, lambda e, ptv=ptv, nb=nb: e.activation(out=As[:, 0:nb, :], in_=ptv[:, 0:nb, :], func=AF.Copy), [As], [pt])
                            kb.op(DVE, lambda e, ptv=ptv, nb=nb: e.tensor_copy(out=Ks[:, 0:nb, :], in_=ptv[:, 4:4 + nb, :]), [Ks], [pt])
                            if STOP == 4301:
                                raise _Stop()
                            pt = psp.get()
                            ptv = pt[:].rearrange("p (c t) -> p c t", c=8)

                            def f(e, ptv=ptv, chs=chs):
                                for ci, ch in enumerate(chs):
                                    cs = slice(ch * CH, (ch + 1) * CH)
                                    qmm(e, lambda sl: ptv[sl, ci, :], lambda sl: nBT[sl, cs], lambda sl: ident_b[sl, sl])
                                    i = qmm(e, lambda sl: ptv[sl, 4 + ci, :], lambda sl: AR[sl, 0, cs], lambda sl: nBT[sl, cs])
                                return i
                            kb.op(PE, f, [pt], [AR, nBT, cstb])
                            if STOP == 4302:
                                raise _Stop()
                            kb.op(ACT, lambda e, ptv=ptv, nb=nb: e.activation(out=nBs[:, 0:nb, :], in_=ptv[:, 0:nb, :], func=AF.Copy), [nBs], [pt])
                            if STOP == 4303:
                                raise _Stop()
                            if STOP == 4308:
                                kb.op(DVE, lambda e, ptv=ptv, nb=nb: e.tensor_copy(out=X1[:, 0:nb, :], in_=ptv[:, 4:4 + nb, :]), [X1], [pt, nBs])
                                raise _Stop()
                            if STOP == 4307:
                                dbt = carv.get([128, 8, CH], F32)
                                kb.op(ACT, lambda e, ptv=ptv: e.activation(out=dbt[:], in_=ptv[:, :, :], func=AF.Copy), [dbt], [pt])
                                kb.dma(SP, dbg_res, [(dbg_ap[:, 0:512], dbt[:].rearrange('p c t -> p (c t)'))], ins=[dbt])
                                dbt2 = carv.get([128, 4, CH], F32)
                                kb.op(ACT, lambda e: e.activation(out=dbt2[:, 0, :], in_=AR[:, 0, 0:CH], func=AF.Copy), [dbt2], [AR])
                                kb.op(ACT, lambda e: e.activation(out=dbt2[:, 1, :], in_=nBT[:, 0:CH], func=AF.Copy), [dbt2], [nBT])
                                kb.op(ACT, lambda e: e.activation(out=dbt2[:, 2, :], in_=KhT[:, 0:CH], func=AF.Copy), [dbt2], [KhT])
                                kb.op(ACT, lambda e: e.activation(out=dbt2[:, 3, :], in_=AR[:, 1, 0:CH], func=AF.Copy), [dbt2], [AR])
                                kb.dma(SP, dbg_res, [(dbg_ap[:, 512:768], dbt2[:].rearrange('p c t -> p (c t)'))], ins=[dbt2], waw=False)
                                raise _Stop()
                            if STOP == 4305:
                                kb.op(DVE, lambda e, ptv=ptv, nb=nb: e.tensor_copy(out=Ks[:, 0:nb, :], in_=ptv[:, 4:4 + nb, :]), [Ks], [pt])
                                raise _Stop()
                            if STOP == 4306:
                                kb.op(ACT, lambda e, ptv=ptv, nb=nb: e.activation(out=X1[:, 0:nb, :], in_=ptv[:, 4:4 + nb, :], func=AF.Copy), [X1], [pt])
                                raise _Stop()
                            if STOP == 4304:
                                kb.op(DVE, lambda e, ptv=ptv, nb=nb: e.tensor_copy(out=X1[:, 0:nb, :], in_=ptv[:, 4:4 + nb, :]), [X1], [pt])
                                raise _Stop()
                            kb.op(DVE, lambda e, ptv=ptv, nb=nb, m_x1=m_x1: e.tensor_tensor(out=X1[:, 0:nb, :], in0=ptv[:, 4:4 + nb, :], in1=mask(m_x1, nb), op=ALU.mult), [X1], [pt, cst])
                            if STOP == 431:
                                raise _Stop()
                            pg2 = psp.get()
                            pg2v = pg2[:].rearrange("p (c s t) -> p c s t", c=4, s=2)
                            pg3 = psp.get()
                            pg3v = pg3[:].rearrange("p (c s t) -> p c s t", c=4, s=2)

                            def f(e, pg2v=pg2v, pg3v=pg3v, chs=chs):
                                for ci, ch in enumerate(chs):
                                    cs = slice(ch * CH, (ch + 1) * CH)
                                    qmm(e, lambda sl: pg2v[sl, ci, :, :], lambda sl: nBT[sl, cs], lambda sl: AR[sl, :, cs])
                                    i = qmm(e, lambda sl: pg3v[sl, ci, :, :], lambda sl: KhT[sl, cs], lambda sl: AR[sl, :, cs])
                                return i
                            kb.op(PE, f, [pg2, pg3], [AR, nBT, KhT])
                            XT = XT_p.get()
                            kb.op(DVE, lambda e, pg2v=pg2v, nb=nb, XT=XT, m_T=m_T: e.tensor_tensor(out=XT[:, 0:nb, 0, :], in0=pg2v[:, 0:nb, 0, :], in1=mask(m_T, nb), op=ALU.mult), [XT], [pg2, cst])
                            kb.op(DVE, lambda e, pg2v=pg2v, nb=nb, m_Ti=m_Ti: e.tensor_tensor(out=G2s[:, 0:nb, 1, :], in0=pg2v[:, 0:nb, 1, :], in1=mask(m_Ti, nb), op=ALU.mult), [G2s], [pg2, cst])
                            kb.op(DVE, lambda e, pg3v=pg3v, nb=nb, m_T=m_T: e.tensor_tensor(out=G3s[:, 0:nb, 0, :], in0=pg3v[:, 0:nb, 0, :], in1=mask(m_T, nb), op=ALU.mult), [G3s], [pg3, cst])
                            kb.op(DVE, lambda e, pg3v=pg3v, nb=nb, m_Ti=m_Ti: e.tensor_tensor(out=G3s[:, 0:nb, 1, :], in0=pg3v[:, 0:nb, 1, :], in1=mask(m_Ti, nb), op=ALU.mult), [G3s], [pg3, cst])
                            if STOP == 432:
                                raise _Stop()
                            kb.op(POOL, lambda e, nb=nb, XT=XT: e.tensor_tensor(out=XT[:, 0:nb, 1, :], in0=XT[:, 0:nb, 0, :], in1=mask(CO_ID64, nb), op=ALU.add), [XT], [XT, cst])
                            if STOP == 433:
                                raise _Stop()
                            Xc = X1
                            for rnd in range(6):
                                last = (rnd == 5)
                                pa = psp.get()
                                pav = pa[:].rearrange("p (c s t) -> p c s t", c=4, s=2)
                                if rnd == 0:
                                    def f(e, pav=pav, Xc=Xc, XT=XT, nb=nb):
                                        for ci in range(nb):
                                            i = qmm(e, lambda sl: pav[sl, ci, 0, :], lambda sl: Xc[sl, ci, :], lambda sl: XT[sl, ci, 0, :])
                                        return i
                                elif not last:
                                    def f(e, pav=pav, Xc=Xc, XT=XT, nb=nb):
                                        for ci in range(nb):
                                            i = qmm(e, lambda sl: pav[sl, ci, :, :], lambda sl: Xc[sl, ci, :], lambda sl: XT[sl, ci, :, :])
                                        return i
                                else:
                                    def f(e, pav=pav, Xc=Xc, XT=XT, nb=nb):
                                        for ci in range(nb):
                                            i = qmm(e, lambda sl: pav[sl, ci, 1, :], lambda sl: Xc[sl, ci, :], lambda sl: XT[sl, ci, 1, :])
                                        return i
                                kb.op(PE, f, [pa], [Xc, XT])
                                XTn = XT_p.get()
                                if not last:
                                    pb = psp.get()
                                    pbv = pb[:].rearrange("p (c t) -> p c t", c=8)

                                    def fb(e, pbv=pbv, Xc=Xc, XT=XT, nb=nb):
                                        for ci in range(nb):
                                            i = qmm(e, lambda sl: pbv[sl, ci, :], lambda sl: XT[sl, ci, 0, :], lambda sl: Xc[sl, ci, :])
                                        return i
                                    kb.op(PE, fb, [pb], [Xc, XT])
                                    Xn = Xc_p.get()
                                    kb.op(ACT, lambda e, pbv=pbv, Xn=Xn, nb=nb: e.activation(out=Xn[:, 0:nb, :], in_=pbv[:, 0:nb, :], func=AF.Copy), [Xn], [pb])
                                    kb.op(ACT, lambda e, pav=pav, XTn=XTn, nb=nb: e.activation(out=XTn[:, 0:nb, 0, :], in_=pav[:, 0:nb, 0, :], func=AF.Copy), [XTn], [pa])
                                if rnd == 0:
                                    kb.op(DVE, lambda e, XT=XT, XTn=XTn, nb=nb: e.tensor_copy(out=XTn[:, 0:nb, 1, :], in_=XT[:, 0:nb, 1, :]), [XTn], [XT])
                                else:
                                    kb.op(DVE, lambda e, pav=pav, XT=XT, XTn=XTn, nb=nb: e.tensor_tensor(out=XTn[:, 0:nb, 1, :], in0=pav[:, 0:nb, 1, :], in1=XT[:, 0:nb, 1, :], op=ALU.add), [XTn], [pa, XT])
                                XT = XTn
                                if not last:
                                    Xc = Xn
                            if STOP == 434:
                                raise _Stop()
                            pl = psp.get()
                            plv = pl[:].rearrange("p (c t) -> p c t", c=8)

                            def f(e, plv=plv, chs=chs, XT=XT):
                                for ci, ch in enumerate(chs):
                                    qmm(e, lambda sl: plv[sl, ci, :], lambda sl: G3s[sl, ci, 0, :], lambda sl: Vs[sl, ch, :])
                                    i = qmm(e, lambda sl: plv[sl, 4 + ci, :], lambda sl: As[sl, ci, :], lambda sl: XT[sl, ci, 1, :])
                                return i
                            kb.op(PE, f, [pl], [G3s, Vs, As, XT])
                            kb.op(ACT, lambda e, plv=plv, nb=nb: e.activation(out=LkVs[:, 0:nb, :], in_=plv[:, 0:nb, :], func=AF.Copy), [LkVs], [pl])
                            kb.op(DVE, lambda e, plv=plv, nb=nb: e.tensor_copy(out=PTs[:, 0:nb, :], in_=plv[:, 4:4 + nb, :]), [PTs], [pl])
                            if STOP == 44:
                                raise _Stop()
                            for ci, ch in enumerate(chs):
                                cs = slice(ch * CH, (ch + 1) * CH)
                                pu = psp.get()

                                def f(e, pu=pu, ci=ci, XT=XT, ST=ST):
                                    qmm(e, lambda sl: pu[sl, 0:CH], lambda sl: XT[sl, ci, 1, :], lambda sl: LkVs[sl, ci, :], start=True, stop=False)
                                    return qmm(e, lambda sl: pu[sl, 0:CH], lambda sl: PTs[sl, ci, :], lambda sl: ST[sl, :], start=False, stop=True)
                                kb.op(PE, f, [pu], [XT, LkVs, PTs, ST])
                                Us = Us_p.get()
                                kb.op(ACT, lambda e, pu=pu, Us=Us: e.activation(out=Us[:], in_=pu[:, 0:CH], func=AF.Copy), [Us], [pu])
                                pss = psp.get()

                                def f(e, pss=pss, ci=ci, ch=ch, ST=ST, Us=Us):
                                    qmm(e, lambda sl: pss[sl, 0:CH], lambda sl: ident_b[sl, sl], lambda sl: ST[sl, :], start=True, stop=False)
                                    qmm(e, lambda sl: pss[sl, 0:CH], lambda sl: Ks[sl, ci, :], lambda sl: Vs[sl, ch, :], start=False, stop=False)
                                    return qmm(e, lambda sl: pss[sl, 0:CH], lambda sl: nBs[sl, ci, :], lambda sl: Us[sl, :], start=False, stop=True)
                                kb.op(PE, f, [pss], [ST, Ks, Vs, nBs, Us, cstb])
                                py = psp.get()

                                def f(e, py=py, ci=ci, ch=ch, cs=cs, ST=ST, Us=Us):
                                    qmm(e, lambda sl: py[sl, 0:CH], lambda sl: AR[sl, 1, cs], lambda sl: ST[sl, :], start=True, stop=False)
                                    qmm(e, lambda sl: py[sl, 0:CH], lambda sl: G3s[sl, ci, 1, :], lambda sl: Vs[sl, ch, :], start=False, stop=False)
                                    return qmm(e, lambda sl: py[sl, 0:CH], lambda sl: G2s[sl, ci, 1, :], lambda sl: Us[sl, :], start=False, stop=True)
                                kb.op(PE, f, [py], [AR, ST, G3s, Vs, G2s, Us])
                                STn = ST_p.get()
                                kb.op(DVE, lambda e, pss=pss, STn=STn, ch=ch: e.tensor_scalar(out=STn[:], in0=pss[:, 0:CH], scalar1=wc[:, ch:ch + 1], scalar2=None, op0=ALU.mult), [STn], [pss, wc])
                                ST = STn
                                if z == 0:
                                    kb.op(ACT, lambda e, py=py, ch=ch: e.activation(out=yacc[:, ch, :], in_=py[:, 0:CH], func=AF.Copy), [yacc], [py])
                                else:
                                    kb.op(DVE, lambda e, py=py, ch=ch: e.tensor_tensor(out=yacc[:, ch, :], in0=py[:, 0:CH], in1=yacc[:, ch, :], op=ALU.add), [yacc], [py, yacc])
                    mean = gst[:, 0, :]
                    var = gst[:, 1, :]
                    kb.op(DVE, lambda e: e.tensor_reduce(out=mean, in_=yacc[:], axis=AX.X, op=ALU.add), [gst], [yacc])
                    kb.op(DVE, lambda e: e.tensor_scalar(out=mean, in0=mean, scalar1=1.0 / CH, scalar2=None, op0=ALU.mult), [gst], [gst])
                    kb.op(DVE, lambda e: e.tensor_tensor(out=yacc[:], in0=yacc[:], in1=mean.unsqueeze(2).to_broadcast([128, NCH, CH]), op=ALU.subtract), [yacc], [yacc, gst])
                    kb.op(POOL, lambda e: e.tensor_tensor(out=f1[:, 0:T], in0=yacc[:].rearrange("p c t -> p (c t)"), in1=yacc[:].rearrange("p c t -> p (c t)"), op=ALU.mult), [f1], [yacc])
                    kb.op(DVE, lambda e: e.tensor_reduce(out=var, in_=f1[:, 0:T].rearrange("p (c t) -> p c t", t=CH), axis=AX.X, op=ALU.add), [gst], [f1])
                    kb.op(DVE, lambda e: e.tensor_scalar(out=var, in0=var, scalar1=1.0 / CH, scalar2=GN_EPS, op0=ALU.mult, op1=ALU.add), [gst], [gst])
                    kb.op(ACT, lambda e: e.activation(out=var, in_=var, func=AF.Sqrt), [gst], [gst])
                    kb.op(DVE, lambda e: e.reciprocal(out=var, in_=var), [gst], [gst])
                    kb.op(DVE, lambda e: e.tensor_tensor(out=yn[:], in0=yacc[:], in1=var.unsqueeze(2).to_broadcast([128, NCH, CH]), op=ALU.mult), [yn], [yacc, gst])
                    for c0 in range(0, NCH, 8):
                        ncc = min(8, NCH - c0)
                        pt = psp.get()
                        ptv = pt[:].rearrange("p (c t) -> p c t", c=8)

                        def f(e, ptv=ptv, c0=c0, ncc=ncc):
                            for cc in range(ncc):
                                i = qmm(e, lambda sl: ptv[sl, cc, :], lambda sl: yn[sl, c0 + cc, :], lambda sl: ident_b[sl, sl])
                            return i
                        kb.op(PE, f, [pt], [yn, cstb])
                        kb.op(DVE, lambda e, ptv=ptv, c0=c0, ncc=ncc, j=j: e.tensor_scalar(out=f3t[:, c0 * CH:(c0 + ncc) * CH].rearrange("p (c t) -> p c t", t=CH), in0=ptv[:, 0:ncc, :],
                                                                                       scalar1=pv[:, PV_GNG + j:PV_GNG + j + 1], scalar2=pv[:, PV_GNB + j:PV_GNB + j + 1], op0=ALU.mult, op1=ALU.add), [f3t], [pt, pv])
                    kb.op(POOL, lambda e: e.tensor_tensor(out=f3t[:], in0=f3t[:], in1=bonus[:], op=ALU.add), [f3t], [f3t, bonus])
                    kb.op(DVE, lambda e: e.tensor_tensor(out=kk[:], in0=f3t[:], in1=gT[:], op=ALU.mult), [kk], [f3t, gT])
                    kb.dma(SP, brT, [(brT[2, j], kk[:])], ins=[kk], waw=False)

                if STOP == 5 and b == STOPB:
                    raise _Stop()
                phase_begin()
                NS = 256
                wg = carv.get([128, DC, 3 * D], BF16)
                wo3 = carv.get([128, 3, 4, D], BF16)
                wo_ = carv.get([128, DC, D], BF16)
                for g in range(6):
                    load_w(wg, wg[:, :, g * 512:(g + 1) * 512], w_in[l][:, 4128 + g * 512:4128 + (g + 1) * 512].rearrange("(k p) n -> p k n", p=128))
                for bi, wsrc in enumerate((w_o_attn, w_o_conv, w_o_rwkv)):
                    load_w(wo3, wo3[:, bi, :, :], wsrc[l].rearrange("(k p) n -> p k n", p=128))
                for g in range(2):
                    load_w(wo_, wo_[:, :, g * 512:(g + 1) * 512], w_out[l][:, g * 512:(g + 1) * 512].rearrange("(k p) n -> p k n", p=128))
                hs = carv.get([128, DC, NS], BF16)
                brs = carv.get([128, 3, 4, NS], BF16)
                xs_p = Pool([carv.get([128, DC, NS], F32) for _ in range(2)])
                mg = carv.get([128, DC, NS], BF16)
                mtmp = carv.get([128, NS], F32)
                sg_p = Pool([carv.get([128, NS], F32) for _ in range(2)])
                sqt = carv.get([128, DC, NS], F32)
                stat = carv.get([128, 4, NS], F32)
                x1 = carv.get([128, DC, NS], F32)
                h2 = carv.get([128, DC, NS], BF16)
                for t0 in range(0, T, NS):
                    n = NS
                    col = 2 if t0 < TC else b
                    kb.dma(SP, hs, [(hs[:, :, 0:n], hT_d[:, :, t0:t0 + n].rearrange("c p t -> p c t"))], ins=[hT_d])
                    kb.dma(SP, brs, [(brs[:, bi, :, 0:n], brT[bi, :, :, t0:t0 + n].rearrange("c p t -> p c t")) for bi in range(3)], ins=[brT])
                    xs = xs_p.get()
                    kb.dma(SP, xs, [(xs[:, :, 0:n], xT[b][:, :, t0:t0 + n].rearrange("c p t -> p c t"))], ins=[xT[b]])
                    for dc in range(DC):
                        for bi in range(3):
                            pg = psp.get()

                            def f(e, pg=pg, bi=bi, dc=dc, n=n):
                                for kc in range(DC):
                                    i = e.matmul(pg[:, 0:n], lhsT=wg[:, kc, bi * D + dc * 128:bi * D + (dc + 1) * 128], rhs=hs[:, kc, 0:n], start=(kc == 0), stop=(kc == DC - 1))
                                for kc in range(4):
                                    i = e.matmul(pg[:, 256:256 + n], lhsT=wo3[:, bi, kc, dc * 128:(dc + 1) * 128], rhs=brs[:, bi, kc, 0:n], start=(kc == 0), stop=(kc == 3))
                                return i
                            kb.op(PE, f, [pg], [wg, hs, wo3, brs])
                            sg = sg_p.get()
                            kb.op(ACT, lambda e, pg=pg, sg=sg, n=n: e.activation(out=sg[:, 0:n], in_=pg[:, 0:n], func=AF.Sigmoid), [sg], [pg])
                            if bi == 0:
                                kb.op(DVE, lambda e, pg=pg, sg=sg, n=n: e.tensor_tensor(out=mtmp[:, 0:n], in0=pg[:, 256:256 + n], in1=sg[:, 0:n], op=ALU.mult), [mtmp], [pg, sg])
                            else:
                                kb.op(DVE, lambda e, pg=pg, sg=sg, n=n: e.tensor_tensor(out=sg[:, 0:n], in0=pg[:, 256:256 + n], in1=sg[:, 0:n], op=ALU.mult), [sg], [pg, sg])
                                if bi == 1:
                                    kb.op(POOL, lambda e, sg=sg, n=n: e.tensor_tensor(out=mtmp[:, 0:n], in0=mtmp[:, 0:n], in1=sg[:, 0:n], op=ALU.add), [mtmp], [mtmp, sg])
                                else:
                                    kb.op(POOL, lambda e, sg=sg, n=n, dc=dc: e.tensor_tensor(out=mg[:, dc, 0:n], in0=mtmp[:, 0:n], in1=sg[:, 0:n], op=ALU.add), [mg], [mtmp, sg])
                    for dc in range(DC):
                        po = psp.get()

                        def f(e, po=po, dc=dc, n=n):
                            for kc in range(DC):
                                i = e.matmul(po[:, 0:n], lhsT=wo_[:, kc, dc * 128:(dc + 1) * 128], rhs=mg[:, kc, 0:n], start=(kc == 0), stop=(kc == DC - 1))
                            return i
                        kb.op(PE, f, [po], [wo_, mg])
                        kb.op(DVE, lambda e, po=po, dc=dc, n=n, xs=xs, col=col: e.scalar_tensor_tensor(out=xs[:, dc, 0:n], in0=po[:, 0:n], scalar=modv[:, 16 + dc, col:col + 1], in1=xs[:, dc, 0:n], op0=ALU.mult, op1=ALU.add), [xs], [po, modv, xs])
                    ln_block(xs, n, PV_LN1G, PV_LN1B, x1, sqt, stat)
                    kb.dma(SP, xT[b], [(xT[b][:, :, t0:t0 + n].rearrange("c p t -> p c t"), x1[:, :, 0:n])], ins=[x1], waw=False)
                    for dc in range(DC):
                        E = DVE if dc % 2 else POOL
                        kb.op(E, lambda e, dc=dc, n=n, col=col: e.tensor_scalar(out=h2[:, dc, 0:n], in0=x1[:, dc, 0:n], scalar1=modv[:, 32 + dc, col:col + 1],
                                                                           scalar2=modv[:, 24 + dc, col:col + 1], op0=ALU.mult, op1=ALU.add), [h2], [x1, modv])
                    kb.dma(SP, h2T_d, [(h2T_d[:, :, t0:t0 + n].rearrange("c p t -> p c t"), h2[:, :, 0:n])], ins=[h2], waw=False)

                if STOP == 6 and b == STOPB:
                    raise _Stop()
                HF = DFF // 2
                HFC = FC // 2
                for half in range(2):
                    phase_begin()
                    w13 = carv.get([128, DC, 2 * HF], BF16)
                    w2 = carv.get([128, HFC, D], BF16)
                    for part in range(2):
                        for g in range(2):
                            c0 = part * DFF + half * HF + g * 704
                            load_w(w13, w13[:, :, part * HF + g * 704:part * HF + (g + 1) * 704], ffn_w13[l][:, c0:c0 + 704].rearrange("(k p) n -> p k n", p=128))
                    for g in range(2):
                        load_w(w2, w2[:, :, g * 512:(g + 1) * 512], ffn_w2[l][half * HF:(half + 1) * HF, g * 512:(g + 1) * 512].rearrange("(k p) n -> p k n", p=128))
                    if STOP == 61:
                        raise _Stop()
                    h2s_p = Pool([carv.get([128, DC, NS], BF16) for _ in range(2)])
                    x1s_p = Pool([carv.get([128, DC, NS], F32) for _ in range(2)])
                    fT = carv.get([128, HFC, NS], BF16)
                    su_p = Pool([carv.get([128, NS], F32) for _ in range(2)])
                    part_p = Pool([carv.get([128, DC, NS], F32) for _ in range(2)])
                    if half == 1:
                        sqt = carv.get([128, DC, NS], F32)
                        stat = carv.get([128, 4, NS], F32)
                        xo = carv.get([128, DC, NS], F32)
                    for t0 in range(0, T, NS):
                        n = NS
                        col = 2 if t0 < TC else b
                        h2s = h2s_p.get()
                        kb.dma(SP, h2s, [(h2s[:], h2T_d[:, :, t0:t0 + n].rearrange("c p t -> p c t"))], ins=[h2T_d])
                        pt_ = part_p.get()
                        if half == 1:
                            x1s = x1s_p.get()
                            kb.dma(SP, x1s, [(x1s[:], xT[b][:, :, t0:t0 + n].rearrange("c p t -> p c t"))], ins=[xT[b]])
                            kb.dma(SP, pt_, [(pt_[:], o2p[:, :, t0:t0 + n].rearrange("c p t -> p c t"))], ins=[o2p])
                        for fc in range(HFC):
                            pu = psp.get()

                            def f(e, pu=pu, fc=fc, h2s=h2s, n=n):
                                for kc in range(DC):
                                    e.matmul(pu[:, 0:n], lhsT=w13[:, kc, fc * 128:(fc + 1) * 128], rhs=h2s[:, kc, :], start=(kc == 0), stop=(kc == DC - 1))
                                for kc in range(DC):
                                    i = e.matmul(pu[:, 256:256 + n], lhsT=w13[:, kc, HF + fc * 128:HF + (fc + 1) * 128], rhs=h2s[:, kc, :], start=(kc == 0), stop=(kc == DC - 1))
                                return i
                            kb.op(PE, f, [pu], [w13, h2s])
                            su = su_p.get()
                            kb.op(ACT, lambda e, pu=pu, su=su, n=n: e.activation(out=su[:, 0:n], in_=pu[:, 0:n], func=AF.Silu), [su], [pu])
                            kb.op(DVE, lambda e, pu=pu, su=su, fc=fc, n=n: e.tensor_tensor(out=fT[:, fc, 0:n], in0=pu[:, 256:256 + n], in1=su[:, 0:n], op=ALU.mult), [fT], [pu, su])
                        if STOP == 615:
                            raise _Stop()
                        for dc in range(DC):
                            po = psp.get()

                            def f(e, po=po, dc=dc, n=n):
                                for kc in range(HFC):
                                    i = e.matmul(po[:, 0:n], lhsT=w2[:, kc, dc * 128:(dc + 1) * 128], rhs=fT[:, kc, 0:n], start=(kc == 0), stop=(kc == HFC - 1))
                                return i
                            kb.op(PE, f, [po], [w2, fT])
                            if half == 0:
                                E = evac_engine()
                                kb.op(E, copy_op(E, pt_[:, dc, 0:n], po[:, 0:n]), [pt_], [po])
                            else:
                                kb.op(DVE, lambda e, po=po, dc=dc, n=n, x1s=x1s, col=col: e.scalar_tensor_tensor(out=x1s[:, dc, 0:n], in0=po[:, 0:n], scalar=modv[:, 40 + dc, col:col + 1], in1=x1s[:, dc, 0:n], op0=ALU.mult, op1=ALU.add), [x1s], [po, modv, x1s])
                                kb.op(DVE, lambda e, pt_=pt_, dc=dc, n=n, x1s=x1s, col=col: e.scalar_tensor_tensor(out=x1s[:, dc, 0:n], in0=pt_[:, dc, 0:n], scalar=modv[:, 40 + dc, col:col + 1], in1=x1s[:, dc, 0:n], op0=ALU.mult, op1=ALU.add), [x1s], [pt_, modv, x1s])
                        if STOP == 62:
                            raise _Stop()
                        if half == 0:
                            kb.dma(SP, o2p, [(o2p[:, :, t0:t0 + n].rearrange("c p t -> p c t"), pt_[:, :, 0:n])], ins=[pt_], waw=False)
                        else:
                            ln_block(x1s, n, PV_LN2G, PV_LN2B, xo, sqt, stat)
                            kb.dma(SP, xT[b], [(xT[b][:, :, t0:t0 + n].rearrange("c p t -> p c t"), xo[:, :, 0:n])], ins=[xo], waw=False)
                    if STOP == 63 and half == 0:
                        raise _Stop()
                if STOP == 64 and b == STOPB:
                    raise _Stop()

        try:
            layers()
        except _Stop:
            pass
        phase_begin()
        xi_p = Pool([carv.get([128, DC, 128], F32) for _ in range(2)])
        xo_p = Pool([carv.get([128, D], F32) for _ in range(2)])
        for b in range(2):
            for ti in range(TL // 128):
                t0 = TC + ti * 128
                xi = xi_p.get()
                kb.dma(SP, xi, [(xi[:], xT[b][:, :, t0:t0 + 128].rearrange("c p t -> p c t"))], ins=[xT[b]])
                xo = xo_p.get()
                for half in range(2):
                    pt = psp.get()

                    def f(e, pt=pt, xi=xi, half=half):
                        for q in range(4):
                            i = e.matmul(pt[:, q * 128:(q + 1) * 128], lhsT=xi[:, half * 4 + q, :], rhs=ident_f, start=True, stop=True)
                        return i
                    kb.op(PE, f, [pt], [xi, cst])
                    E = evac_engine()
                    kb.op(E, copy_op(E, xo[:, half * 512:(half + 1) * 512], pt[:]), [xo], [pt])
                kb.dma(SP, out_res, [(out_ap[b, ti * 128:(ti + 1) * 128, :], xo[:])], ins=[xo], waw=False)
        kb.finish([out_res] + ([dbg_res] if dbg_ap is not None and dbg_res.w is not None else []))
        print('nsem', kb.nsem, 'cnt', {k: getattr(kb, k).cnt for k in ('PE', 'ACT', 'DVE', 'POOL')})
    return nc


def host_prep(inputs, TL, DEPTH):
    f = lambda a: np.ascontiguousarray(np.asarray(a, dtype=np.float32))
    L = DEPTH
    T = TC + TL
    g = {k: f(v) for k, v in inputs.items()}
    consts = np.zeros((128, NCONST), np.float32)
    consts[:, CO_ID:CO_ID + 128] = np.eye(128)
    consts[:, CO_ONES:CO_ONES + 128] = 1.0
    for h in range(2):
        consts[h * 64:(h + 1) * 64, CO_BONES + h * 64:CO_BONES + (h + 1) * 64] = 1.0
    r = np.arange(128)[:, None] % 64
    c = np.arange(64)[None, :]
    consts[:, CO_MSL:CO_MSL + 64] = (r > c)
    consts[:, CO_MSU:CO_MSU + 64] = (r < c)
    consts[:, CO_MIL:CO_MIL + 64] = (r >= c)
    consts[:, CO_MIU:CO_MIU + 64] = (r <= c)
    consts[:, CO_ID64:CO_ID64 + 64] = (r == c)
    rope = np.zeros((128, 4, T), np.float32)
    tt = np.arange(TL)
    row = (tt // 64).astype(np.float32)
    colp = (tt % 64).astype(np.float32)
    inv = (10000.0 ** (-np.arange(0, 16, 2, dtype=np.float32) / 16)).astype(np.float32)
    cosf = np.ones((32, T), np.float32)
    sinf = np.zeros((32, T), np.float32)
    for i in range(32):
        pos = row if i < 16 else colp
        ang = (pos * inv[i % 8]).astype(np.float32)
        cosf[i, TC:] = np.cos(ang)
        s = np.sin(ang)
        sinf[i, TC:] = -s if (i % 16) < 8 else s
    rope[64:96, 0] = cosf * SCALE
    rope[64:96, 1] = sinf * SCALE
    rope[64:96, 2] = cosf
    rope[64:96, 3] = sinf
    perm96 = np.arange(96)
    for i in range(32):
        perm96[64 + i] = 64 + (i + 8 if (i % 16) < 8 else i - 8)
    perm768 = np.concatenate([h * 96 + perm96 for h in range(8)])
    w_uq_sw = np.ascontiguousarray(g["w_uq"][:, :, perm768])
    kr = g["w_in"][:, :, 640:672]
    perm32 = perm96[64:] - 64
    w_kr = np.zeros((g["w_in"].shape[0], D, 2, 96), np.float32)
    w_kr[:, :, 0, 64:] = kr
    w_kr[:, :, 1, 64:] = kr[:, :, perm32]
    w_kr = w_kr.reshape(-1, D, 192)

    def fm(v, nchunk):
        return v.reshape(v.shape[0], nchunk, 128).transpose(0, 2, 1)
    pvec = np.zeros((g["mod_b"].shape[0], 128, NPV), np.float32)
    pvec[:, :, PV_MODB:PV_MODB + 48] = fm(g["mod_b"], 48)
    pvec[:, :, PV_LN1G:PV_LN1G + 8] = fm(g["ln1_g"], 8)
    pvec[:, :, PV_LN1B:PV_LN1B + 8] = fm(g["ln1_b"], 8)
    pvec[:, :, PV_LN2G:PV_LN2G + 8] = fm(g["ln2_g"], 8)
    pvec[:, :, PV_LN2B:PV_LN2B + 8] = fm(g["ln2_b"], 8)
    pvec[:, :, PV_QN:PV_QN + 3] = fm(g["q_norm"], 3)
    pvec[:, :, PV_KVN:PV_KVN + 2] = fm(g["kv_norm"], 2)
    cw = g["conv_w"]
    for j in range(4):
        for tap in range(3):
            pvec[:, :, PV_CONV + j * 3 + tap] = cw[:, tap, j * 128:(j + 1) * 128]
    pvec[:, :, PV_MU:PV_MU + 15] = fm(g["rw_mu"], 15)
    for z in range(2):
        pvec[:, :, PV_W0 + z * 4:PV_W0 + z * 4 + 4] = fm(g["rw_w0"][:, z], 4)
        pvec[:, :, PV_A0 + z * 4:PV_A0 + z * 4 + 4] = fm(g["rw_a0"][:, z], 4)
    pvec[:, :, PV_KK:PV_KK + 4] = fm(g["rw_k_k"], 4)
    pvec[:, :, PV_KA:PV_KA + 4] = fm(g["rw_k_a"], 4)
    pvec[:, :, PV_RK:PV_RK + 4] = fm(g["rw_r_k"], 4)
    pvec[:, :, PV_GNG:PV_GNG + 4] = fm(g["rw_gn_g"], 4)
    pvec[:, :, PV_GNB:PV_GNB + 4] = fm(g["rw_gn_b"], 4)
    shared = {
        "consts": consts, "rope": rope, "pvec": pvec[:L],
        "mod_w": g["mod_w"][:L], "w_in": g["w_in"][:L], "w_kr": w_kr[:L], "w_uq": g["w_uq"][:L], "w_uq_sw": w_uq_sw[:L],
        "w_ukv": g["w_ukv"][:L], "w_o_attn": g["w_o_attn"][:L], "w_o_conv": g["w_o_conv"][:L], "w_o_rwkv": g["w_o_rwkv"][:L],
        "w_out": g["w_out"][:L], "ffn_w13": g["ffn_w13"][:L], "ffn_w2": g["ffn_w2"][:L],
        "rw_w_up": g["rw_w_up"][:L].reshape(L, 128, 512), "rw_a_up": g["rw_a_up"][:L].reshape(L, 128, 512), "rw_g_up": g["rw_g_up"][:L],
    }
    shared = {k: np.ascontiguousarray(v) for k, v in shared.items()}
    in_maps = []
    ncores = g["x"].shape[0] // 2
    for ci in range(ncores):
        cc = np.stack([g["c"][2 * ci], g["c"][2 * ci + 1], g["c_ctx"]], axis=-1)
        cT = np.ascontiguousarray(cc.reshape(DC, 128, 3).transpose(1, 0, 2))
        m = dict(shared)
        m["x"] = np.ascontiguousarray(g["x"][2 * ci:2 * ci + 2])
        m["ctx"] = np.ascontiguousarray(g["ctx"][2 * ci:2 * ci + 2])
        m["cT"] = cT
        in_maps.append(m)
    return in_maps


_CACHE = {}


def kernel(**inputs):
    TL = inputs["x"].shape[1]
    DEPTH = inputs["mod_w"].shape[0]
    key = (TL, DEPTH)
    in_maps = host_prep(inputs, TL, DEPTH)
    nc = build(TL, DEPTH)
    res = run_bass_kernel_spmd(nc, in_maps, core_ids=list(range(len(in_maps))))
    global LAST_RES
    LAST_RES = res
    out = np.concatenate([r["out"] for r in res.results], axis=0)
    return out.astype(np.float32)
```
